# Optimizing a Trainium2 kernel written in Bass

```python
import jax, jax.numpy as jnp
from jax import lax
import numpy as np

D_MODEL = 1024
BATCH = 8
SEQ = 2048
DEPTH = 1
DEC_BATCH = 128
DEC_SEQ = 1
PAST_LEN = 16384
PAGE_SIZE = 128

PLE_DIM = 256
MIX_WIDTH = D_MODEL
GLA_HEADS = 4
GLA_WIDTH = MIX_WIDTH // 2
GLA_DV = GLA_WIDTH // GLA_HEADS
GLA_DK = GLA_DV // 2
GLA_RANK = 16
GLA_GATE_NORM = 16.0
GLA_CHUNK = 64
CONV_WIDTH = MIX_WIDTH - GLA_WIDTH
CONV_GROUPS = 8
CONV_K = 3
D_FF = ((8 * D_MODEL // 3 + 127) // 128) * 128
EPS = 1e-6
N_NORMS = 8
SPLIT_SIZES = (GLA_HEADS * GLA_DK, GLA_HEADS * GLA_DK, GLA_WIDTH, GLA_WIDTH, GLA_RANK, CONV_WIDTH, CONV_WIDTH, CONV_WIDTH)
IN_COLS = sum(SPLIT_SIZES)

kernel_name = "hymba_gla_shortconv_macaron_decoder_step"


def rms_norm(x, g):
    x32 = x.astype(jnp.float32)
    y = x32 * lax.rsqrt(jnp.mean(x32 * x32, axis=-1, keepdims=True) + EPS)
    return (y * g.astype(jnp.float32)).astype(x.dtype)


def swiglu(x, w_gate, w_up, w_down):
    return (jax.nn.silu(x @ w_gate) * (x @ w_up)) @ w_down


def gla_chunked(q, k, v, log_a, s0):
    out_dtype = v.dtype
    q, k, v, log_a, s0 = (a.astype(jnp.float32) for a in (q, k, v, log_a, s0))
    bsz, t_len, h, dk = q.shape
    dv = v.shape[-1]
    c = min(GLA_CHUNK, t_len)
    n = -(-t_len // c)
    pad = n * c - t_len

    def blocks(a):
        a = jnp.pad(a, ((0, 0), (0, pad), (0, 0), (0, 0)))
        return a.reshape(bsz, n, c, h, a.shape[-1])

    qb, kb, vb, lb = blocks(q), blocks(k), blocks(v), blocks(log_a)
    bcum = jnp.cumsum(lb, axis=2)
    to_scan = lambda a: jnp.moveaxis(a, 1, 0)
    mask = jnp.tril(jnp.ones((c, c), dtype=bool))[None, :, :, None, None]

    def step(s, inp):
        qc, kc, vc, bc = inp
        o_inter = jnp.einsum('bthk,bhkv->bthv', qc * jnp.exp(bc), s)
        diff = jnp.where(mask, bc[:, :, None] - bc[:, None, :], -jnp.inf)
        att = jnp.sum(qc[:, :, None] * kc[:, None, :] * jnp.exp(diff), axis=-1)
        o_intra = jnp.einsum('btsh,bshv->bthv', att, vc)
        b_last = bc[:, -1]
        k_dec = kc * jnp.exp(b_last[:, None] - bc)
        s_new = jnp.exp(b_last)[..., None] * s + jnp.einsum('bshk,bshv->bhkv', k_dec, vc)
        return s_new, o_inter + o_intra

    s_fin, o = lax.scan(step, s0, (to_scan(qb), to_scan(kb), to_scan(vb), to_scan(bcum)))
    o = jnp.moveaxis(o, 0, 1).reshape(bsz, n * c, h, dv)[:, :t_len]
    return o.astype(out_dtype), s_fin.astype(out_dtype)


def token_mixer(u, s0, conv0, w_in, w_gate_up, b_gate, gla_norm_g, conv_w, w_out):
    bsz, t_len, _ = u.shape
    z = u @ w_in
    points, acc = [], 0
    for sz in SPLIT_SIZES[:-1]:
        acc += sz
        points.append(acc)
    q, k, v, g, r, cb, cc, ch = jnp.split(z, points, axis=-1)
    q = q.reshape(bsz, t_len, GLA_HEADS, GLA_DK) * (GLA_DK ** -0.5)
    k = k.reshape(bsz, t_len, GLA_HEADS, GLA_DK)
    v = v.reshape(bsz, t_len, GLA_HEADS, GLA_DV)
    gate_logit = (r @ w_gate_up + b_gate).astype(jnp.float32)
    log_a = (jax.nn.log_sigmoid(gate_logit) / GLA_GATE_NORM).reshape(bsz, t_len, GLA_HEADS, GLA_DK)
    o, s_new = gla_chunked(q, k, v, log_a, s0)
    o = rms_norm(o, gla_norm_g) * jax.nn.silu(g.reshape(bsz, t_len, GLA_HEADS, GLA_DV))
    o = o.reshape(bsz, t_len, GLA_WIDTH)
    uc = cc * ch
    full = jnp.concatenate([conv0.astype(uc.dtype), uc], axis=1)
    y = sum(conv_w[j] * full[:, j:j + t_len] for j in range(CONV_K))
    oc = cb * y
    conv_new = full[:, -(CONV_K - 1):]
    out = jnp.concatenate([o, oc], axis=-1) @ w_out
    return out, s_new, conv_new


def decoder_layer(x, p, s0, conv0, ng, w_in, w_gate_up, b_gate, gla_norm_g, conv_w, w_out,
                  f1_gate, f1_up, f1_down, f2_gate, f2_up, f2_down, w_ple_proj, w_ple_gate):
    h = x + 0.5 * rms_norm(swiglu(rms_norm(x, ng[0]), f1_gate, f1_up, f1_down), ng[1])
    mix, s_new, conv_new = token_mixer(rms_norm(h, ng[2]), s0, conv0, w_in, w_gate_up, b_gate,
                                       gla_norm_g, conv_w, w_out)
    h = h + rms_norm(mix, ng[3])
    h = h + 0.5 * rms_norm(swiglu(rms_norm(h, ng[4]), f2_gate, f2_up, f2_down), ng[5])
    gate = jax.nn.sigmoid(rms_norm(h, ng[6]) @ w_ple_gate)
    h = h + rms_norm((p @ w_ple_proj) * gate, ng[7])
    return h, s_new, conv_new


def setup_inputs(seed: int = 0) -> dict:
    key = jax.random.key(seed)
    ks = jax.random.split(key, 24)
    nrm = lambda k, shape, scale: jax.random.normal(k, shape, jnp.float32) * scale
    D = D_MODEL
    return {
        "x_prompt": nrm(ks[0], (BATCH, SEQ, D), 1.0),
        "x_sample": nrm(ks[1], (DEC_BATCH, DEC_SEQ, D), 1.0),
        "state_gla": nrm(ks[2], (DEPTH, DEC_BATCH, GLA_HEADS, GLA_DK, GLA_DV), 1.0),
        "state_conv": nrm(ks[3], (DEPTH, DEC_BATCH, CONV_K - 1, CONV_WIDTH), 1.0),
        "p_prompt": nrm(ks[4], (DEPTH, BATCH, SEQ, PLE_DIM), 1.0),
        "p_sample": nrm(ks[5], (DEPTH, DEC_BATCH, DEC_SEQ, PLE_DIM), 1.0),
        "norm_g": 1.0 + nrm(ks[6], (DEPTH, N_NORMS, D), 0.05),
        "w_in": nrm(ks[7], (DEPTH, D, IN_COLS), D ** -0.5),
        "w_gate_up": nrm(ks[8], (DEPTH, GLA_RANK, GLA_HEADS * GLA_DK), GLA_RANK ** -0.5),
        "b_gate": nrm(ks[9], (DEPTH, GLA_HEADS * GLA_DK), 0.1),
        "gla_norm_g": 1.0 + nrm(ks[10], (DEPTH, GLA_DV), 0.05),
        "conv_w": nrm(ks[11], (DEPTH, CONV_K, CONV_WIDTH), CONV_K ** -0.5),
        "w_out": nrm(ks[12], (DEPTH, MIX_WIDTH, D), MIX_WIDTH ** -0.5),
        "ffn1_gate": nrm(ks[13], (DEPTH, D, D_FF), D ** -0.5),
        "ffn1_up": nrm(ks[14], (DEPTH, D, D_FF), D ** -0.5),
        "ffn1_down": nrm(ks[15], (DEPTH, D_FF, D), D_FF ** -0.5),
        "ffn2_gate": nrm(ks[16], (DEPTH, D, D_FF), D ** -0.5),
        "ffn2_up": nrm(ks[17], (DEPTH, D, D_FF), D ** -0.5),
        "ffn2_down": nrm(ks[18], (DEPTH, D_FF, D), D_FF ** -0.5),
        "w_ple_proj": nrm(ks[19], (DEPTH, PLE_DIM, D), PLE_DIM ** -0.5),
        "w_ple_gate": nrm(ks[20], (DEPTH, D, D), D ** -0.5),
    }


def reference(x_prompt, x_sample, state_gla, state_conv, p_prompt, p_sample, norm_g, w_in,
              w_gate_up, b_gate, gla_norm_g, conv_w, w_out, ffn1_gate, ffn1_up, ffn1_down,
              ffn2_gate, ffn2_up, ffn2_down, w_ple_proj, w_ple_gate):
    hp, hs = x_prompt, x_sample
    gla_p, conv_p, gla_s, conv_s = [], [], [], []
    for i in range(DEPTH):
        weights = (norm_g[i], w_in[i], w_gate_up[i], b_gate[i], gla_norm_g[i], conv_w[i], w_out[i],
                   ffn1_gate[i], ffn1_up[i], ffn1_down[i], ffn2_gate[i], ffn2_up[i], ffn2_down[i],
                   w_ple_proj[i], w_ple_gate[i])
        s0_p = jnp.zeros((hp.shape[0], GLA_HEADS, GLA_DK, GLA_DV), hp.dtype)
        c0_p = jnp.zeros((hp.shape[0], CONV_K - 1, CONV_WIDTH), hp.dtype)
        hp, sp, cp = decoder_layer(hp, p_prompt[i], s0_p, c0_p, *weights)
        hs, ss, cs = decoder_layer(hs, p_sample[i], state_gla[i], state_conv[i], *weights)
        gla_p.append(sp); conv_p.append(cp); gla_s.append(ss); conv_s.append(cs)
    return (hp, hs, jnp.stack(gla_p), jnp.stack(conv_p), jnp.stack(gla_s), jnp.stack(conv_s))
```

```python
import contextlib
import numpy as np
import concourse.bass as bass
import concourse.mybir as mybir
from concourse.bass_utils import run_bass_kernel_spmd

F32 = mybir.dt.float32
F32R = mybir.dt.float32r
AF = mybir.ActivationFunctionType
ALU = mybir.AluOpType

NCORES = 8
D = 1024
SEQ = 2048
NS = 16
DFF = 2816
NJ = DFF // 128
PLE = 256
EPS = 1e-6
TT = 512
NTOK = SEQ + NS
SLOT_W = 2048
NSLOT = 5
COMPUTE = ("pe", "act", "dve", "pool")

C_ID, C_MK, C_ONE, C_EPS = 0, 128, 256, 384
C_W = 392
C_OD, C_OV, C_UN, C_LN, C_NI = 0, 128, 256, 384, 512
CR_W = 640


class Ref:
    __slots__ = ("base", "lo", "hi", "ap")

    def __init__(self, base, lo, hi, ap):
        self.base, self.lo, self.hi, self.ap = base, lo, hi, ap


class Buf:
    def __init__(self, name, t, P, W):
        self.name, self.t, self.P, self.W = name, t, P, W

    def v(self, lo, hi, p0=0, p1=None, r=False, a=None, sl=None):
        p1 = self.P if p1 is None else p1
        ap = self.t[p0:p1, lo:hi]
        if r:
            ap = ap.bitcast(F32R)
        if a is not None:
            ap = ap.rearrange("p (a b) -> p a b", a=a)
            if sl is not None:
                ap = ap[:, :, sl[0]:sl[1]]
        return Ref(self.name, lo, hi, ap)


class Prog:
    def __init__(self, nc, stack):
        self.nc, self.stack = nc, stack
        self.streams = {e: [] for e in ("pe", "act", "dve", "pool", "sp")}
        self.seq = {e: 0 for e in COMPUTE}
        self.esem = {e: stack.enter_context(nc.semaphore("s_" + e)) for e in COMPUTE}
        self.dsem, self.dcount = {}, {}
        self.known = {e: {} for e in self.streams}
        self.recs = {}
        self.out_tokens = []
        self.psum_names = set()

    def sbuf(self, name, P, W):
        t = self.stack.enter_context(self.nc.sbuf_tensor("sb_" + name, [P, W], F32))
        return Buf(name, t, P, W)

    def psum(self, name):
        t = self.stack.enter_context(self.nc.psum_tensor("ps_" + name, [128, 512], F32))
        self.psum_names.add(name)
        return Buf(name, t, 128, 512)

    def _deps_read(self, r, deps, eng):
        ps = r.base in self.psum_names
        for rec in self.recs.get(r.base, []):
            if ps or (rec[0] < r.hi and r.lo < rec[1]):
                if rec[2] is not None:
                    deps.add(rec[2])
                if ps:
                    for k_, v_ in rec[3].items():
                        if not (k_[0] == "e" and k_[1] == eng):
                            deps.add((k_[0], k_[1], v_))

    def _deps_write(self, w, deps):
        ps = w.base in self.psum_names
        for rec in self.recs.get(w.base, []):
            if ps or (rec[0] < w.hi and w.lo < rec[1]):
                if rec[2] is not None:
                    deps.add(rec[2])
                for k_, v_ in rec[3].items():
                    deps.add((k_[0], k_[1], v_))

    def _reg_read(self, r, tok):
        covered = False
        for rec in self.recs.setdefault(r.base, []):
            if rec[0] < r.hi and r.lo < rec[1]:
                k_ = (tok[0], tok[1])
                rec[3][k_] = max(rec[3].get(k_, 0), tok[2])
                covered = True
        if not covered:
            self.recs[r.base].append([r.lo, r.hi, None, {(tok[0], tok[1]): tok[2]}])

    def _reg_write(self, w, tok):
        lst = self.recs.setdefault(w.base, [])
        lst[:] = [rec for rec in lst if not (w.lo <= rec[0] and rec[1] <= w.hi)]
        lst.append([w.lo, w.hi, tok, {}])

    def op(self, eng, fn, reads=(), writes=(), dma=None, ndma=1, is_output=False):
        deps = set()
        for r in reads:
            self._deps_read(r, deps, eng)
        for w in writes:
            self._deps_write(w, deps)
        if dma is None:
            self.seq[eng] += 1
            tok = ("e", eng, self.seq[eng])
        else:
            if dma not in self.dsem:
                self.dsem[dma] = self.stack.enter_context(self.nc.semaphore("d_" + dma))
                self.dcount[dma] = 0
            self.dcount[dma] += 16 * ndma
            tok = ("d", dma, self.dcount[dma])
        need = {}
        for d in deps:
            key = (d[0], d[1])
            if d[0] == "e" and d[1] == eng and eng == "pe":
                continue
            if self.known[eng].get(key, 0) >= d[2]:
                continue
            need[key] = max(need.get(key, 0), d[2])
        for key, val in need.items():
            self.known[eng][key] = val
        self.streams[eng].append((sorted(need.items()), fn, tok, ndma))
        for r in reads:
            self._reg_read(r, tok)
        for w in writes:
            self._reg_write(w, tok)
        if is_output:
            self.out_tokens.append(tok)
        return tok

    def finish_waits(self, eng="sp"):
        need = {}
        for d in self.out_tokens:
            key = (d[0], d[1])
            need[key] = max(need.get(key, 0), d[2])
        self.streams[eng].append((sorted(need.items()), None, None, 0))

    def emit(self, block):
        prog = self

        def sem(key):
            return prog.esem[key[1]] if key[0] == "e" else prog.dsem[key[1]]

        def run(engname, e):
            for waits, fn, tok, ndma in prog.streams[engname]:
                for key, val in waits:
                    e.wait_ge(sem(key), val)
                if fn is None:
                    continue
                res = fn(e)
                if tok[0] == "e":
                    ins = res[-1] if isinstance(res, (list, tuple)) else res
                    ins.then_inc(prog.esem[tok[1]], 1)
                else:
                    lst = res if isinstance(res, (list, tuple)) else [res]
                    assert len(lst) == ndma
                    for ins in lst:
                        ins.then_inc(prog.dsem[tok[1]], 16)

        @block.tensor
        def _(e):
            run("pe", e)

        @block.scalar
        def _(e):
            run("act", e)

        @block.vector
        def _(e):
            run("dve", e)

        @block.gpsimd
        def _(e):
            run("pool", e)

        @block.sync
        def _(e):
            run("sp", e)


def build_program():
    nc = bass.Bass("TRN2", target_bir_lowering=False)

    def din(name, shape):
        return nc.dram_tensor(name, list(shape), F32, kind="ExternalInput").ap()

    def dout(name, shape):
        return nc.dram_tensor(name, list(shape), F32, kind="ExternalOutput").ap()

    x_d = din("x", (D, NTOK))
    p_d = din("p", (PLE, NTOK))
    sg_d = din("sg", (128, 2 * NS * 128))
    sc_d = din("sc", (128, 4 * 2 * NS))
    cst_d = din("cst", (128, C_W))
    csr_d = din("csr", (128, CR_W))
    ngl_d = din("ngl", (128, 64))
    sm_d = din("sm", (128, 16))
    wgu_d = din("wgu", (16, 256))
    bg_d = din("bg", (1, 256))
    wr_d = din("wr", (128, 128))
    fA_d = [din(f"f{f}a", (NJ * 128, 2048)) for f in (1, 2)]
    fB_d = [din(f"f{f}b", (16 * 128, 1408)) for f in (1, 2)]
    wk_d = din("wk", (128, 2048))
    wv_d = din("wv", (2 * 128, 2048))
    wca_d = din("wca", (4 * 128, 2048))
    wcb_d = din("wcb", (4 * 128, 1024))
    wkq_d = din("wkq", (2 * 128, 2048))
    wg_d = din("wg", (2 * 128, 2048))
    wo_d = din("wo", (4 * 128, 2048))
    wpp_d = din("wpp", (4 * 128, 512))
    wpg_d = din("wpg", (4 * 128, 2048))

    y_d = dout("y", (D, NTOK))
    glap_d = dout("gla_p", (2, 128, 128))
    convp_d = dout("conv_p", (2, 512))
    glas_d = dout("gla_s", (2, 128, NS * 128))
    convs_d = dout("conv_s", (NS, 1024))

    with contextlib.ExitStack() as stack:
        P = Prog(nc, stack)
        h = P.sbuf("h", 128, 8 * TT)
        xn = P.sbuf("xn", 128, 8 * TT)
        sq = P.sbuf("sq", 128, 8 * TT)
        om = P.sbuf("om", 128, 8 * TT)
        big = P.sbuf("big", 128, 13312)
        stgs = [P.sbuf(f"stg{i}", 128, D) for i in range(4)]
        qtz = P.sbuf("qtz", 128, 4 * TT)
        cst = P.sbuf("cst", 128, C_W)
        csr = P.sbuf("csr", 128, CR_W)
        ngl = P.sbuf("ngl", 128, 64)
        sm = P.sbuf("sm", 128, 16)
        nglh = P.sbuf("nglh", 128, 64)
        wgu = P.sbuf("wgu", 16, 256)
        bg = P.sbuf("bg", 1, 256)
        wrs = P.sbuf("wrs", 128, 128)
        Sst = P.sbuf("Sst", 128, 256)
        Sr = [P.sbuf(f"Sr{i}", 128, 256) for i in range(2)]
        carry = P.sbuf("carry", 128, 8)
        rs = P.sbuf("rs", 128, TT)
        junk = P.sbuf("junk", 128, 8)
        hgb = [P.sbuf(f"hgb{i}", 128, TT) for i in range(2)]
        hs = P.sbuf("hs", 128, 8 * NS)
        rss = P.sbuf("rss", 128, 2 * NS)
        tmpa = [P.sbuf(f"tmpa{i}", 128, TT) for i in range(2)]
        tmpb = [P.sbuf(f"tmpb{i}", 128, TT) for i in range(2)]
        slots = [P.sbuf(f"wslot{i}", 128, SLOT_W) for i in range(NSLOT)]
        banks = [P.psum(f"bank{i}") for i in range(8)]
        block = stack.enter_context(nc.Block())

        st_ = {"bank": 0, "ta": 0, "tb": 0, "stg": 0}

        def nb():
            b = banks[st_["bank"] % 8]
            st_["bank"] += 1
            return b

        def ta():
            t = tmpa[st_["ta"] % 2]
            st_["ta"] += 1
            return t

        def tb():
            t = tmpb[st_["tb"] % 2]
            st_["tb"] += 1
            return t

        def dma_in(eng, name, dst, src, is_output=False):
            P.op(eng, lambda e: e.dma_start(out=dst.ap, in_=src), writes=[dst], dma=name)

        def dma_out(name, dst_ap, src):
            P.op("sp", lambda e: e.dma_start(out=dst_ap, in_=src.ap), reads=[src], dma=name, is_output=True)

        def mm(out, pairs, extra=(), first=True, last=True):
            def fn(e):
                n_ = len(pairs)
                for i, (a, b) in enumerate(pairs):
                    ins = e.matmul(out.ap, a.ap, b.ap, start=(first and i == 0), stop=(last and i == n_ - 1))
                return ins
            rd = []
            for a, b in pairs:
                rd.append(a)
                rd.append(b)
            P.op("pe", fn, reads=rd + list(extra), writes=[out])

        def mm_multi(groups, out_cover):
            def fn(e):
                for out, pairs in groups:
                    n_ = len(pairs)
                    for i, (a, b) in enumerate(pairs):
                        ins = e.matmul(out.ap, a.ap, b.ap, start=(i == 0), stop=(i == n_ - 1))
                return ins
            rd = []
            for out, pairs in groups:
                for a, b in pairs:
                    rd.append(a)
                    rd.append(b)
            P.op("pe", fn, reads=rd, writes=[out_cover])

        def transposes(items, out_cover, extra_reads):
            def fn(e):
                for out, in_, idn in items:
                    ins = e.transpose(out.ap, in_.ap, idn.ap)
                return ins
            rd = list(extra_reads)
            P.op("pe", fn, reads=rd, writes=[out_cover])

        def act(out, in_, func, extra=(), **kw):
            P.op("act", lambda e: e.activation(out=out.ap, in_=in_.ap, func=func, **kw), reads=[in_] + list(extra), writes=[out])

        def tt(out, in0, in1, op, eng="dve"):
            P.op(eng, lambda e: e.tensor_tensor(out=out.ap, in0=in0.ap, in1=in1.ap, op=op), reads=[in0, in1], writes=[out])

        def stt(out, in0, scalar, in1, op0, op1):
            sc = scalar.ap if isinstance(scalar, Ref) else scalar
            rd = [in0, in1] + ([scalar] if isinstance(scalar, Ref) else [])
            P.op("dve", lambda e: e.scalar_tensor_tensor(out=out.ap, in0=in0.ap, scalar=sc, in1=in1.ap, op0=op0, op1=op1),
                 reads=rd, writes=[out])

        def memset(buf_ref, val=0.0):
            P.op("dve", lambda e: e.memset(buf_ref.ap, val), writes=[buf_ref])

        wq = {"order": [], "issued": 0, "used": 0}

        def weight_order():
            def ffn_w(f):
                return [(fA_d[f], j, 2048) for j in range(NJ)] + [(fB_d[f], mh, 1408) for mh in range(16)]

            def mix_w():
                o = [(wk_d, 0, 2048), (wv_d, 0, 2048), (wv_d, 1, 2048), (wkq_d, 0, 2048), (wkq_d, 1, 2048), (wg_d, 0, 2048), (wg_d, 1, 2048)]
                for c in range(4):
                    o += [(wca_d, c, 2048), (wcb_d, c, 1024)]
                o += [(wo_d, m2, 2048) for m2 in range(4)]
                return o

            def ple_w():
                o = []
                for m2 in range(4):
                    o += [(wpg_d, m2, 2048), (wpp_d, m2, 512)]
                return o
            o = []
            for ti in range(SEQ // TT):
                last = ti == SEQ // TT - 1
                o += ffn_w(0) + mix_w() + (mix_w() if last else []) + ffn_w(1) + ple_w() + (ple_w() if last else [])
            return o

        wq["order"] = weight_order()

        def wissue():
            i = wq["issued"]
            d, blk, words = wq["order"][i]
            sl = slots[i % NSLOT]
            dst = sl.v(0, words, r=True)
            src = d[blk * 128:(blk + 1) * 128, :]
            P.op("pool", lambda e: e.dma_start(out=dst.ap, in_=src), writes=[dst], dma=f"ws{i % NSLOT}")
            wq["issued"] += 1

        def wget(d, blk, cap=None):
            i = wq["used"]
            assert wq["order"][i][0] is d and wq["order"][i][1] == blk, (i, blk)
            lim = i + NSLOT - 1
            if cap is not None:
                lim = min(lim, cap + 1)
            while wq["issued"] < min(len(wq["order"]), lim):
                wissue()
            wq["used"] += 1
            return slots[i % NSLOT]

        dma_in("sp", "c_cst", cst.v(0, C_W), cst_d[:, :])
        dma_in("sp", "c_ngl", ngl.v(0, 64), ngl_d[:, :])
        dma_in("sp", "c_sm", sm.v(0, 16), sm_d[:, :])
        dma_in("sp", "c_bg", bg.v(0, 256), bg_d[:, :])
        P.op("pool", lambda e: e.dma_start(out=csr.v(0, CR_W, r=True).ap, in_=csr_d[:, :]), writes=[csr.v(0, CR_W)], dma="c_csr")
        P.op("pool", lambda e: e.dma_start(out=wgu.v(0, 256, r=True).ap, in_=wgu_d[:, :]), writes=[wgu.v(0, 256)], dma="c_wgu")
        P.op("pool", lambda e: e.dma_start(out=wrs.v(0, 128, r=True).ap, in_=wr_d[:, :]), writes=[wrs.v(0, 128)], dma="c_wr")
        ZW = 256
        zt = P.sbuf("zt", 128, ZW)
        memset(zt.v(0, ZW))

        def zero_r(ref_fn, lo, hi):
            for a_ in range(lo, hi, ZW):
                b_ = min(hi, a_ + ZW)
                o_ = ref_fn(a_, b_)
                P.op("dve", (lambda o, i: lambda e: e.tensor_copy(out=o.ap, in_=i.ap))(o_, zt.v(0, b_ - a_)),
                     reads=[zt.v(0, b_ - a_)], writes=[o_])
        zero_r(lambda a_, b_: qtz.v(a_, b_, r=True), 0, 4 * TT)
        memset(Sst.v(0, 256))
        zero_r(lambda a_, b_: Sr[0].v(a_, b_, r=True), 0, 256)
        memset(carry.v(0, 8))
        P.op("dve", lambda e: e.tensor_scalar(out=nglh.v(0, 64).ap, in0=ngl.v(0, 64).ap, scalar1=0.5, scalar2=0.0, op0=ALU.mult, op1=ALU.add),
             reads=[ngl.v(0, 64)], writes=[nglh.v(0, 64)])
        ident = lambda k=128: cst.v(C_ID, C_ID + k, 0, k)
        eps_ap = cst.v(C_EPS, C_EPS + 1)
        eps_ap_row = cst.v(C_EPS, C_EPS + 1, 0, 1)

        def nstg():
            b_ = stgs[2 + st_["stg"] % 2]
            st_["stg"] += 1
            return b_

        xq = {"n": 0, "pending": {}}

        def x_issue(t0, n, s):
            rows = min(n, 128)
            stg = stgs[xq["n"] % 2]
            xq["n"] += 1
            dst = stg.v(0, D, 0, rows)
            src = x_d[t0 + s * rows:t0 + (s + 1) * rows, :]
            P.op("sp", (lambda d_, s_: lambda e: e.dma_start(out=d_.ap, in_=s_))(dst, src), writes=[dst], dma="in_" + stg.name)
            xq["pending"][(t0, s)] = stg

        def _hcA(c, n):
            return h.v(c * TT, c * TT + n)

        def _hcB(c, n):
            return stgs[c // 2].v((c % 2) * 512, (c % 2) * 512 + n)

        def _hcS(c, n):
            return hs.v(c * NS, c * NS + n)

        CUR = {"hcf": _hcA, "name": "A"}

        def hc(c, n):
            return CUR["hcf"](c, n)

        def norm_stats(n, nch, ones_col):
            bk = nb()
            mm(bk.v(0, n), [(csr.v(ones_col, ones_col + 128, r=True), sq.v(c * TT, c * TT + n, r=True)) for c in range(nch)])
            t = ta()
            act(t.v(0, n), bk.v(0, n), AF.Ln, extra=[eps_ap], bias=eps_ap.ap)
            act(rs.v(0, n), t.v(0, n), AF.Exp, scale=-0.5)

        def ew(eng, kind, **kw):
            if kind == "stt":
                out, in0, scalar, in1, op0, op1 = kw["out"], kw["in0"], kw["scalar"], kw["in1"], kw["op0"], kw["op1"]
                sc = scalar.ap if isinstance(scalar, Ref) else scalar
                rd = [in0, in1] + ([scalar] if isinstance(scalar, Ref) else [])
                P.op(eng, lambda e: e.scalar_tensor_tensor(out=out.ap, in0=in0.ap, scalar=sc, in1=in1.ap, op0=op0, op1=op1),
                     reads=rd, writes=[out])
            else:
                out, in0, in1, op = kw["out"], kw["in0"], kw["in1"], kw["op"]
                P.op(eng, lambda e: e.tensor_tensor(out=out.ap, in0=in0.ap, in1=in1.ap, op=op), reads=[in0, in1], writes=[out])

        NDVE = 8

        def pre_norm(g, n):
            for c in range(8):
                act(sq.v(c * TT, c * TT + n, r=True), hc(c, n), AF.Square)
            for c in range(NDVE, 8):
                P.op("pool", (lambda o, i, w: lambda e: e.tensor_scalar(out=o.ap, in0=i.ap, scalar1=w.ap, scalar2=1.0, op0=ALU.mult, op1=ALU.mult))(
                    xn.v(c * TT, c * TT + n, r=True), hc(c, n), ngl.v(g * 8 + c, g * 8 + c + 1)),
                    reads=[hc(c, n), ngl.v(g * 8 + c, g * 8 + c + 1)], writes=[xn.v(c * TT, c * TT + n)])
            norm_stats(n, 8, C_OD)
            for c in range(NDVE):
                ew("dve", "stt", out=xn.v(c * TT, c * TT + n, r=True), in0=hc(c, n),
                   scalar=ngl.v(g * 8 + c, g * 8 + c + 1), in1=rs.v(0, n), op0=ALU.mult, op1=ALU.mult)
            for c in range(NDVE, 8):
                ew("pool", "tt", out=xn.v(c * TT, c * TT + n, r=True), in0=xn.v(c * TT, c * TT + n), in1=rs.v(0, n), op=ALU.mult)

        def pre_norm_deferred(g, n, stats=True, done=False, dve_scale=False):
            for c in (() if done else range(8)):
                act(sq.v(c * TT, c * TT + n, r=True), hc(c, n), AF.Square)
                gcol = ngl.v(g * 8 + c, g * 8 + c + 1)
                if dve_scale:
                    P.op("dve", (lambda o, i, w: lambda e: e.tensor_scalar(out=o.ap, in0=i.ap, scalar1=w.ap, scalar2=0.0, op0=ALU.mult, op1=ALU.add))(
                        xn.v(c * TT, c * TT + n, r=True), hc(c, n), gcol), reads=[hc(c, n), gcol], writes=[xn.v(c * TT, c * TT + n)])
                else:
                    P.op("act", (lambda o, i, w: lambda e: e.activation(out=o.ap, in_=i.ap, func=AF.Identity, scale=w.ap))(
                        xn.v(c * TT, c * TT + n, r=True), hc(c, n), gcol), reads=[hc(c, n), gcol], writes=[xn.v(c * TT, c * TT + n)])
            if stats:
                norm_stats(n, 8, C_OD)

        def fo_evac(g, fo, m, bk, n, scale, col0=0):
            gcol = (nglh if scale == 0.5 else ngl).v(g * 8 + m, g * 8 + m + 1)
            P.op("act", (lambda o, i, w: lambda e: e.activation(out=o.ap, in_=i.ap, func=AF.Identity, scale=w.ap))(
                fo.v(m * TT, m * TT + n, r=True), bk.v(col0, col0 + n), gcol),
                reads=[bk.v(col0, col0 + n), gcol], writes=[fo.v(m * TT, m * TT + n)])
            act(sq.v(m * TT, m * TT + n, r=True), bk.v(col0, col0 + n), AF.Square)

        def post_norm(fo, n):
            norm_stats(n, 8, C_OD)
            def m1(eng, c):
                ew(eng, "tt", out=fo.v(c * TT, c * TT + n, r=True), in0=fo.v(c * TT, c * TT + n), in1=rs.v(0, n), op=ALU.mult)
            def ad(eng, c):
                ew(eng, "tt", out=hc(c, n), in0=hc(c, n), in1=fo.v(c * TT, c * TT + n), op=ALU.add)
            seq_d = [("m", 0), ("m", 1), ("a", 0), ("m", 2), ("a", 1), ("m", 3), ("a", 2), ("m", 4), ("a", 3), ("m", 5), ("a", 4),
                     ("m", 6), ("a", 5), ("m", 7), ("a", 6), ("a", 7)]
            seq_p = []
            ip = 0
            for i, (k, c) in enumerate(seq_d):
                (m1 if k == "m" else ad)("dve", c)
                if i % 2 == 1 and ip < len(seq_p):
                    k2, c2 = seq_p[ip]; ip += 1
                    (m1 if k2 == "m" else ad)("pool", c2)
            while ip < len(seq_p):
                k2, c2 = seq_p[ip]; ip += 1
                (m1 if k2 == "m" else ad)("pool", c2)

        def x_chunk(c, n, t0):
            dst = hc(c, n)
            src = x_d[c * 128:(c + 1) * 128, t0:t0 + n]
            P.op("sp", (lambda d_, s_: lambda e: e.dma_start(out=d_.ap, in_=s_))(dst, src), writes=[dst], dma="xin%d_%s" % (c, CUR["name"]))

        def stage_in(n, t0):
            for c in range(8):
                x_chunk(c, n, t0)

        class SampFFN:
            XS, SQS, HF = 11264, 11264 + 128, 11264 + 256
            HT, FT = 0, 2816

            def __init__(self, f, gpre, gpost):
                self.f, self.gpre, self.gpost = f, gpre, gpost
                self.rsc = junk.v(4, 5, 0, NS)

            def xs(self, k, r=True):
                return big.v(self.XS + k * NS, self.XS + (k + 1) * NS, r=r)

            def sqs(self, k, r=True):
                return big.v(self.SQS + k * NS, self.SQS + (k + 1) * NS, r=r)

            def hsc(self, c):
                return hs.v(c * NS, (c + 1) * NS)

            def stats(self):
                bk = nb()
                mm(bk.v(0, NS), [(csr.v(C_OD, C_OD + 128, r=True), self.sqs(c)) for c in range(8)])
                act(rss.v(NS, 2 * NS), bk.v(0, NS), AF.Ln, extra=[eps_ap], bias=eps_ap.ap)
                act(rss.v(0, NS), rss.v(NS, 2 * NS), AF.Exp, scale=-0.5)

            def pre(self):
                n = NS
                for c in range(8):
                    act(self.sqs(c), self.hsc(c), AF.Square)
                    gcol = ngl.v(self.gpre * 8 + c, self.gpre * 8 + c + 1)
                    P.op("act", (lambda o, i, w: lambda e: e.activation(out=o.ap, in_=i.ap, func=AF.Identity, scale=w.ap))(
                        self.xs(c), self.hsc(c), gcol), reads=[self.hsc(c), gcol], writes=[self.xs(c)])
                self.stats()
                bk = nb()
                mm(bk.v(0, 1, 0, n), [(rss.v(0, n, 0, 1), cst.v(C_ONE, C_ONE + 1, 0, 1))])
                act(self.rsc, bk.v(0, 1, 0, n), AF.Copy)

            def chunk_a(self, j, sl):
                n = NS
                bk = nb()
                groups = [(bk.v(u * 128, (u + 1) * 128, 0, n),
                           [(self.xs(k), sl.v(u * 1024 + k * 128, u * 1024 + (k + 1) * 128, r=True)) for k in range(8)])
                          for u in range(2)]
                mm_multi(groups, bk.v(0, 256))
                t = ta()
                rsc = self.rsc
                P.op("act", (lambda o, i, w: lambda e: e.activation(out=o.ap, in_=i.ap, func=AF.Silu, scale=w.ap))(
                    t.v(0, 128, 0, n), bk.v(0, 128, 0, n), rsc), reads=[bk.v(0, 128, 0, n), rsc], writes=[t.v(0, 128, 0, n)])
                stt(om.v(self.HT + j * 128, self.HT + (j + 1) * 128, 0, n, r=True), bk.v(128, 256, 0, n), rsc, t.v(0, 128, 0, n), ALU.mult, ALU.mult)

            def post_a(self):
                n = NS
                for g4 in range(0, NJ, 4):
                    js = list(range(g4, min(NJ, g4 + 4)))
                    bk = nb()
                    items = [(bk.v(jj * n, (jj + 1) * n), om.v(self.HT + j * 128, self.HT + (j + 1) * 128, 0, n), ident(n)) for jj, j in enumerate(js)]
                    transposes(items, bk.v(0, len(js) * n), [om.v(self.HT + g4 * 128, self.HT + (g4 + len(js)) * 128), ident()])
                    act(big.v(self.HF + g4 * n, self.HF + (g4 + len(js)) * n, r=True), bk.v(0, len(js) * n), AF.Copy)

            def chunk_b(self, m, half, sl):
                n = NS
                if half == 0:
                    self.bkb = nb()
                bk = self.bkb
                mm(bk.v(0, 128, 0, n), [(big.v(self.HF + (half * 11 + kk) * n, self.HF + (half * 11 + kk + 1) * n, r=True),
                                         sl.v(kk * 128, (kk + 1) * 128, r=True)) for kk in range(11)], first=(half == 0), last=(half == 1))
                if half == 1:
                    act(om.v(self.FT + m * 128, self.FT + (m + 1) * 128, 0, n, r=True), bk.v(0, 128, 0, n), AF.Copy)

            def post_b(self):
                n = NS
                bk = nb()
                items = [(bk.v(m * n, (m + 1) * n), om.v(self.FT + m * 128, self.FT + (m + 1) * 128, 0, n), ident(n)) for m in range(8)]
                transposes(items, bk.v(0, 8 * n), [om.v(self.FT, self.FT + 1024), ident()])
                for m in range(8):
                    gcol = nglh.v(self.gpost * 8 + m, self.gpost * 8 + m + 1)
                    P.op("act", (lambda o, i, w: lambda e: e.activation(out=o.ap, in_=i.ap, func=AF.Identity, scale=w.ap))(
                        self.xs(m), bk.v(m * n, (m + 1) * n), gcol), reads=[bk.v(m * n, (m + 1) * n), gcol], writes=[self.xs(m)])
                    act(self.sqs(m), bk.v(m * n, (m + 1) * n), AF.Square)
                self.stats()
                for c in range(8):
                    tt(self.xs(c), self.xs(c, r=False), rss.v(0, n), ALU.mult)
                    tt(self.hsc(c), self.hsc(c), self.xs(c, r=False), ALU.add)

        def ffn(f, gpre, gpost, n, samp=None, dve_scale=False):
            pre_norm_deferred(gpre, n, stats=False, dve_scale=dve_scale)
            hid = lambda j: big.v(j * TT, j * TT + n, r=True)
            NW = 3
            i0 = wq["used"]
            wsl = [wget(fA_d[f], j, cap=i0 + NSLOT - 1) for j in range(NW)]
            wbk = [(nb(), nb()) for j in range(NW)]
            for k in range(8):
                def fnk(e, k=k):
                    for j in range(NW):
                        for u in range(2):
                            ins = e.matmul(wbk[j][u].v(0, n).ap, wsl[j].v(u * 1024 + k * 128, u * 1024 + (k + 1) * 128, r=True).ap,
                                           xn.v(k * TT, k * TT + n, r=True).ap, start=(k == 0), stop=(k == 7))
                    return ins
                P.op("pe", fnk, reads=[xn.v(k * TT, k * TT + n)] + [wsl[j].v(0, 2048) for j in range(NW)],
                     writes=[wbk[j][u].v(0, n) for j in range(NW) for u in range(2)])
            norm_stats(n, 8, C_OD)
            for j in range(NJ):
                if j < NW:
                    bg_, bu_ = wbk[j]
                else:
                    sl = wget(fA_d[f], j)
                    bg_, bu_ = nb(), nb()
                    mm(bg_.v(0, n), [(sl.v(k * 128, (k + 1) * 128, r=True), xn.v(k * TT, k * TT + n, r=True)) for k in range(8)])
                    mm(bu_.v(0, n), [(sl.v(1024 + k * 128, 1024 + (k + 1) * 128, r=True), xn.v(k * TT, k * TT + n, r=True)) for k in range(8)])
                t, t2 = ta(), tb()
                tt(t.v(0, n), bg_.v(0, n), rs.v(0, n), ALU.mult)
                act(t.v(0, n), t.v(0, n), AF.Silu)
                tt(t2.v(0, n), bu_.v(0, n), rs.v(0, n), ALU.mult)
                tt(hid(j), t2.v(0, n), t.v(0, n), ALU.mult)
                if samp is not None:
                    if j == 0:
                        samp.pre()
                    samp.chunk_a(j, wsl[j] if j < NW else sl)
            if samp is not None:
                samp.post_a()
            act(junk.v(0, 1, 0, 1), eps_ap_row, AF.Ln)
            for m in range(8):
                bk = nb()
                for half in range(2):
                    sl = wget(fB_d[f], 2 * m + half)
                    mm(bk.v(0, n), [(sl.v(kk * 128, (kk + 1) * 128, r=True), hid(half * 11 + kk)) for kk in range(11)],
                       first=(half == 0), last=(half == 1))
                    if samp is not None:
                        samp.chunk_b(m, half, sl)
                fo_evac(gpost, xn, m, bk, n, 0.5)
            post_norm(xn, n)
            if samp is not None:
                samp.post_b()

        def carve(n):
            o = {}
            off = 0
            def take(name, w):
                nonlocal off
                o[name] = (off, off + w)
                off += w
            take("r", max(n, 16))
            take("sp", 4 * 256 if n == TT else 256)
            take("E", 2 * n)
            take("Einv", 2 * n)
            take("kt", 2 * n)
            take("vtm", 4 * 512)
            take("kdec", 4 * 256)
            take("ucx0", n + 2)
            take("ucx1", n + 2)
            if n == TT:
                take("attm0", 512)
                take("attm1", 512)
                take("osq0", 512)
                take("osq1", 512)
            take("ofm", 4 * n)
            if n == NS:
                take("S0", 2 * NS * 128)
                take("vsel0", NS * 128)
                take("vsel1", NS * 128)
                take("qz", 4 * NS + 2)
                take("osqs", 4 * NS)
                take("cso", 4 * 2 * NS)
                take("sc", 4 * 2 * NS)
                take("cstg", 1024)
            assert off <= 13312, off
            return o

        def mixer(n, is_sample, last_prompt):
            L = carve(n)
            R = lambda name, lo=0, hi=None, **kw: big.v(L[name][0] + lo, L[name][0] + (hi if hi is not None else L[name][1] - L[name][0]), **kw)
            nsub = max(1, n // 128)
            rows = min(n, 128)
            xk = lambda k, r=True: xn.v(k * TT, k * TT + n, r=r)
            if is_sample:
                dma_in("pool", "sg", R("S0", r=True), sg_d[:, :])
                dma_in("pool", "scin", R("sc", r=True), sc_d[:, :])
            bk_r = nb()
            for c in range(8):
                act(sq.v(c * TT, c * TT + n, r=True), hc(c, n), AF.Square)
                gcol = ngl.v(2 * 8 + c, 2 * 8 + c + 1)
                hg = hgb[c % 2].v(0, n, r=True)
                P.op("act", (lambda o, i, w: lambda e: e.activation(out=o.ap, in_=i.ap, func=AF.Identity, scale=w.ap))(hg, hc(c, n), gcol),
                     reads=[hc(c, n), gcol], writes=[hg])
                mm(bk_r.v(0, n, 0, 16), [(wrs.v(c * 16, (c + 1) * 16, r=True), hg)], first=(c == 0), last=(c == 7))
            norm_stats(n, 8, C_OD)
            tt(R("r", 0, n, p0=0, p1=16, r=True), bk_r.v(0, n, 0, 16), rs.v(0, n, 0, 16), ALU.mult)
            for c in range(8):
                ew("dve", "stt", out=xn.v(c * TT, c * TT + n, r=True), in0=hc(c, n),
                   scalar=ngl.v(2 * 8 + c, 2 * 8 + c + 1), in1=rs.v(0, n), op0=ALU.mult, op1=ALU.mult)
            for s in range(nsub):
                bk = nb()
                mm(bk.v(0, 256, 0, rows), [(R("r", s * rows, (s + 1) * rows, p0=0, p1=16, r=True), wgu.v(0, 256, r=True)),
                                            (cst.v(C_ONE, C_ONE + rows, 0, 1), bg.v(0, 256))])
                t = ta()
                act(t.v(0, 256, 0, rows), bk.v(0, 256, 0, rows), AF.Exp, scale=-1.0)
                act(R("sp", s * 256, (s + 1) * 256, p0=0, p1=rows, r=True), t.v(0, 256, 0, rows), AF.Ln, bias=1.0)
            for pp in range(2):
                bk = nb()
                groups = []
                for s in range(nsub):
                    lhsT = R("sp", s * 256 + pp * 128, s * 256 + (pp + 1) * 128, p0=0, p1=rows, r=True)
                    rhs = csr.v(C_UN, C_UN + rows, 0, rows, r=True) if not is_sample else csr.v(C_NI, C_NI + rows, 0, rows, r=True)
                    groups.append((bk.v(s * rows, (s + 1) * rows), [(lhsT, rhs)]))
                mm_multi(groups, bk.v(0, n))
                act(R("E", pp * n, (pp + 1) * n, r=True), bk.v(0, n), AF.Exp)
                if not is_sample:
                    act(R("Einv", pp * n, (pp + 1) * n, r=True), bk.v(0, n), AF.Exp, scale=-1.0)
            def s0_decay():
                for pp_ in range(2):
                    for piece in range(4):
                        lo = pp_ * NS * 128 + piece * 512
                        s0 = R("S0", lo, lo + 512)
                        abc = big.t[:, L["E"][0] + pp_ * n + piece * 4: L["E"][0] + pp_ * n + piece * 4 + 4].unsqueeze(2).to_broadcast([128, 4, 128])
                        P.op("dve", (lambda o, i0, i1: lambda e: e.tensor_tensor(out=o.ap, in0=i0.ap, in1=i1, op=ALU.mult))(
                            R("S0", lo, lo + 512, r=True, a=4), R("S0", lo, lo + 512, a=4), abc),
                            reads=[s0, R("E")], writes=[s0])

            def vsel_build(pp_):
                for hh in range(2):
                    hd = 2 * pp_ + hh
                    vs = "vsel%d" % hh
                    P.op("dve", (lambda o, i0, i1: lambda e: e.tensor_tensor(out=o.ap, in0=i0, in1=i1, op=ALU.mult))(
                        R(vs, 0, NS * 128, p0=0, p1=NS, r=True, a=NS),
                        big.t[0:NS, L["vtm"][0] + hd * 128: L["vtm"][0] + (hd + 1) * 128].unsqueeze(1).to_broadcast([NS, NS, 128]),
                        cst.t[0:NS, C_ID:C_ID + NS].unsqueeze(2).to_broadcast([NS, NS, 128])),
                        reads=[R("vtm"), cst.v(C_ID, C_ID + NS)], writes=[R(vs)])

            if is_sample:
                s0_decay()
            slk = wget(wk_d, 0)
            for s in range(nsub):
                tok = lambda k: Ref("xn", k * TT, k * TT + n, xn.t[:, k * TT + s * rows: k * TT + (s + 1) * rows].bitcast(F32R))
                bkk = nb()
                mm(bkk.v(0, 256, 0, rows), [(tok(k), slk.v(k * 256, (k + 1) * 256, r=True)) for k in range(8)])
                if is_sample:
                    act(R("kdec", 0, 256, p0=0, p1=rows, r=True), bkk.v(0, 256, 0, rows), AF.Copy)
                else:
                    bkd = nb()
                    mm(bkd.v(0, 256), [(csr.v(C_LN, C_LN + 128, r=True), R("sp", s * 256, (s + 1) * 256, r=True))])
                    t = ta()
                    act(t.v(0, 256), bkd.v(0, 256), AF.Exp)
                    tt(R("kdec", s * 256, (s + 1) * 256, r=True), bkk.v(0, 256), t.v(0, 256), ALU.mult)
            for h2 in range(2):
                slv = wget(wv_d, h2)
                for s in range(nsub):
                    tok = lambda k: Ref("xn", k * TT, k * TT + n, xn.t[:, k * TT + s * rows: k * TT + (s + 1) * rows].bitcast(F32R))
                    bkv = nb()
                    mm(bkv.v(0, 256, 0, rows), [(tok(k), slv.v(k * 256, (k + 1) * 256, r=True)) for k in range(8)])
                    if h2 == 0:
                        P.op("dve", (lambda o, i: lambda e: e.tensor_copy(out=o.ap, in_=i.ap))(
                            R("vtm", s * 512 + h2 * 256, s * 512 + (h2 + 1) * 256, p0=0, p1=rows, r=True), bkv.v(0, 256, 0, rows)),
                            reads=[bkv.v(0, 256, 0, rows)], writes=[R("vtm", s * 512 + h2 * 256, s * 512 + (h2 + 1) * 256)])
                    else:
                        act(R("vtm", s * 512 + h2 * 256, s * 512 + (h2 + 1) * 256, p0=0, p1=rows, r=True), bkv.v(0, 256, 0, rows), AF.Copy)
            if is_sample:
                vsel_build(0)
            def conv_chunk(c):
                sla = wget(wca_d, c)
                slb = wget(wcb_d, c)
                bh, bc_ = nb(), nb()
                mm(bh.v(0, n), [(sla.v(k * 128, (k + 1) * 128, r=True), xk(k)) for k in range(8)])
                mm(bc_.v(0, n), [(sla.v(1024 + k * 128, 1024 + (k + 1) * 128, r=True), xk(k)) for k in range(8)])
                t = ta()
                act(t.v(0, n), bh.v(0, n), AF.Copy)
                ux = "ucx%d" % (c % 2)
                tt(R(ux, 2, 2 + n, r=True), bc_.v(0, n), t.v(0, n), ALU.mult)
                w0, w1, w2 = (sm.v(1 + c * 3 + j, 2 + c * 3 + j) for j in range(3))
                y1, y2 = tb(), tb()
                if not is_sample:
                    act(R(ux, 0, 2, r=True), carry.v(c * 2, c * 2 + 2), AF.Copy)
                    P.op("act", (lambda o, i, w: lambda e: e.activation(out=o.ap, in_=i.ap, func=AF.Identity, scale=w.ap))(
                        y1.v(0, n), R(ux, 0, n), w0), reads=[R(ux, 0, n), w0], writes=[y1.v(0, n)])
                    stt(y2.v(0, n), R(ux, 1, 1 + n), w1, y1.v(0, n), ALU.mult, ALU.add)
                    stt(y1.v(0, n), R(ux, 2, 2 + n), w2, y2.v(0, n), ALU.mult, ALU.add)
                    act(carry.v(c * 2, c * 2 + 2), R(ux, n, n + 2), AF.Copy)
                else:
                    c0 = R("sc", c * 2 * NS, c * 2 * NS + NS)
                    c1 = R("sc", c * 2 * NS + NS, (c + 1) * 2 * NS)
                    P.op("act", (lambda o, i, w: lambda e: e.activation(out=o.ap, in_=i.ap, func=AF.Identity, scale=w.ap))(
                        y1.v(0, n), c0, w0), reads=[c0, w0], writes=[y1.v(0, n)])
                    stt(y2.v(0, n), c1, w1, y1.v(0, n), ALU.mult, ALU.add)
                    stt(y1.v(0, n), R(ux, 2, 2 + n), w2, y2.v(0, n), ALU.mult, ALU.add)
                    P.op("dve", (lambda o, i: lambda e: e.tensor_copy(out=o.ap, in_=i.ap))(R("cso", c * 2 * NS, c * 2 * NS + NS, r=True), c1),
                         reads=[c1], writes=[R("cso", c * 2 * NS, c * 2 * NS + NS)])
                    P.op("dve", (lambda o, i: lambda e: e.tensor_copy(out=o.ap, in_=i.ap))(R("cso", c * 2 * NS + NS, (c + 1) * 2 * NS, r=True), R(ux, 2, 2 + n)),
                         reads=[R(ux, 2, 2 + n)], writes=[R("cso", c * 2 * NS + NS, (c + 1) * 2 * NS)])
                bb = nb()
                mm(bb.v(0, n), [(slb.v(k * 128, (k + 1) * 128, r=True), xk(k)) for k in range(8)])
                tt(om.v((4 + c) * TT, (4 + c) * TT + n, r=True), bb.v(0, n), y1.v(0, n), ALU.mult)
            slkk = wget(wkq_d, 0)
            for pp in (() if is_sample else range(2)):
                bk = nb()
                mm(bk.v(0, n), [(slkk.v((pp * 8 + k) * 128, (pp * 8 + k + 1) * 128, r=True), xk(k)) for k in range(8)])
                tt(R("kt", pp * n, (pp + 1) * n, r=True), bk.v(0, n), R("Einv", pp * n, (pp + 1) * n), ALU.mult)
            slq = wget(wkq_d, 1)
            if is_sample:
                zero_r(lambda a_, b_: R("qz", a_, b_, r=True), 0, 4 * NS)
            for pp in range(2):
                bk = nb()
                mm(bk.v(0, n), [(slq.v((pp * 8 + k) * 128, (pp * 8 + k + 1) * 128, r=True), xk(k)) for k in range(8)])
                for hh in range(2):
                    hd = 2 * pp + hh
                    p0, p1 = hh * 64, hh * 64 + 64
                    if not is_sample:
                        stt(qtz.v(hd * TT, hd * TT + n, p0, p1, r=True), bk.v(0, n, p0, p1), 0.125, R("E", pp * n, (pp + 1) * n, p0=p0, p1=p1),
                            ALU.mult, ALU.mult)
                    else:
                        P.op("dve", (lambda o, i: lambda e: e.tensor_scalar(out=o.ap, in0=i.ap, scalar1=0.125, scalar2=0.0, op0=ALU.mult, op1=ALU.add))(
                            R("qz", hd * NS, (hd + 1) * NS, p0=p0, p1=p1, r=True), bk.v(0, n, p0, p1)),
                            reads=[bk.v(0, n, p0, p1)], writes=[R("qz", hd * NS, (hd + 1) * NS)])
            for hd in range(4):
                if hd % 2 == 0:
                    slg = wget(wg_d, hd // 2)
                bgk = nb()
                mm(bgk.v(0, n), [(slg.v(((hd % 2) * 8 + k) * 128, ((hd % 2) * 8 + k + 1) * 128, r=True), xk(k)) for k in range(8)])
                act(om.v(hd * TT, hd * TT + n, r=True), bgk.v(0, n), AF.Silu)
                if hd == 3:
                    act(junk.v(0, 1, 0, 1), eps_ap_row, AF.Ln)
            if is_sample:
                for c in range(4):
                    conv_chunk(c)
            if not is_sample:
                for s in range(nsub):
                    c0_, c1_ = s * 128, (s + 1) * 128
                    ba = nb()
                    groups = []
                    for hd in range(4):
                        pp = hd // 2
                        groups.append((ba.v(hd * 128, (hd + 1) * 128),
                                       [(R("kt", pp * n + c0_, pp * n + c1_, r=True), qtz.v(hd * TT + c0_, hd * TT + c1_, r=True))]))
                    mm_multi(groups, ba.v(0, 512))
                    am = "attm%d" % (s % 2)
                    P.op("dve", (lambda o, i, m_: lambda e: e.tensor_tensor(out=o.ap, in0=i.ap, in1=m_, op=ALU.mult))(
                        R(am, 0, 512, r=True, a=4), ba.v(0, 512, a=4), cst.v(C_MK, C_MK + 128).ap.unsqueeze(1).to_broadcast([128, 4, 128])),
                        reads=[ba.v(0, 512), cst.v(C_MK, C_MK + 128)], writes=[R(am, 0, 512)])
                    conv_chunk(s)
                    bo = nb()
                    groups = []
                    for hd in range(4):
                        pp = hd // 2
                        groups.append((bo.v(hd * 128, (hd + 1) * 128),
                                       [(Sr[s % 2].v(pp * 128, (pp + 1) * 128, r=True), qtz.v(hd * TT + c0_, hd * TT + c1_, r=True)),
                                        (R("vtm", s * 512 + hd * 128, s * 512 + (hd + 1) * 128, r=True), R(am, hd * 128, (hd + 1) * 128, r=True))]))
                    mm_multi(groups, bo.v(0, 512))
                    P.op("act", (lambda o, i: lambda e: e.activation(out=o.ap, in_=i.ap, func=AF.Copy))(
                        R("ofm", 0, 4 * n, r=True, a=4, sl=(c0_, c1_)), bo.v(0, 512, a=4)), reads=[bo.v(0, 512)], writes=[R("ofm", 0, 4 * n)])
                    oq = "osq%d" % (s % 2)
                    act(R(oq, 0, 512, r=True), bo.v(0, 512), AF.Square)
                    bu = nb()
                    groups = []
                    for pp in range(2):
                        groups.append((bu.v(pp * 256, (pp + 1) * 256),
                                       [(R("kdec", s * 256 + pp * 128, s * 256 + (pp + 1) * 128, r=True),
                                         R("vtm", s * 512 + pp * 256, s * 512 + (pp + 1) * 256, r=True))]))
                    mm_multi(groups, bu.v(0, 512))
                    for pp in range(2):
                        for hh in range(2):
                            p0, p1 = hh * 64, hh * 64 + 64
                            stt(Sst.v(pp * 128, (pp + 1) * 128, p0, p1), Sst.v(pp * 128, (pp + 1) * 128, p0, p1),
                                R("E", pp * n + c1_ - 1, pp * n + c1_, p0=p0, p1=p1),
                                bu.v(pp * 256 + hh * 128, pp * 256 + (hh + 1) * 128, p0, p1), ALU.mult, ALU.add)
                    act(Sr[(s + 1) % 2].v(0, 256, r=True), Sst.v(0, 256), AF.Copy)
                    bq = nb()
                    mm(bq.v(0, 512), [(csr.v(C_OV, C_OV + 128, r=True), R(oq, 0, 512, r=True))])
                    tl = ta()
                    act(tl.v(0, 512), bq.v(0, 512), AF.Ln, extra=[eps_ap], bias=eps_ap.ap)
                    rsh = tb()
                    act(rsh.v(0, 512), tl.v(0, 512), AF.Exp, scale=-0.5)
                    t2 = ta()
                    P.op("dve", (lambda o, i0, w, i1: lambda e: e.scalar_tensor_tensor(out=o.ap, in0=i0.ap, scalar=w.ap, in1=i1.ap, op0=ALU.mult, op1=ALU.mult))(
                        t2.v(0, 512, a=4), R("ofm", 0, 4 * n, a=4, sl=(c0_, c1_)), sm.v(0, 1), rsh.v(0, 512, a=4)),
                        reads=[R("ofm", 0, 4 * n), sm.v(0, 1), rsh.v(0, 512)], writes=[t2.v(0, 512)])
                    P.op("dve", (lambda o, i0, i1: lambda e: e.tensor_tensor(out=o.ap, in0=i0.ap, in1=i1.ap, op=ALU.mult))(
                        om.v(0, 4 * TT, r=True, a=4, sl=(c0_, c1_)), om.v(0, 4 * TT, a=4, sl=(c0_, c1_)), t2.v(0, 512, a=4)),
                        reads=[om.v(0, 4 * TT), t2.v(0, 512)], writes=[om.v(0, 4 * TT)])
                if last_prompt:
                    dma_out("o_glap", glap_d.rearrange("a p v -> p a v"), Sst.v(0, 256, a=2))
                    bk = nb()
                    items = []
                    for c in range(4):
                        items.append((bk.v(c * 128, (c + 1) * 128, 0, 2), carry.v(c * 2, c * 2 + 2), ident()))
                    transposes(items, bk.v(0, 512), [carry.v(0, 8), ident()])
                    t = tb()
                    act(t.v(0, 512, 0, 2), bk.v(0, 512, 0, 2), AF.Copy)
                    dma_out("o_convp", convp_d[:, :], t.v(0, 512, 0, 2))
            else:
                for pp in range(2):
                    if pp == 1:
                        vsel_build(1)
                    for piece in range(4):
                        bA, bB = nb(), nb()
                        lhsT = R("kdec", pp * 128, (pp + 1) * 128, p0=0, p1=NS, r=True)
                        mm(bA.v(0, 512), [(lhsT, R("vsel0", piece * 512, (piece + 1) * 512, p0=0, p1=NS, r=True))])
                        mm(bB.v(0, 512), [(lhsT, R("vsel1", piece * 512, (piece + 1) * 512, p0=0, p1=NS, r=True))])
                        lo = pp * NS * 128 + piece * 512
                        tt(R("S0", lo, lo + 512, p0=0, p1=64, r=True), R("S0", lo, lo + 512, p0=0, p1=64), bA.v(0, 512, 0, 64), ALU.add)
                        tt(R("S0", lo, lo + 512, p0=64, p1=128, r=True), R("S0", lo, lo + 512, p0=64, p1=128), bB.v(0, 512, 64, 128), ALU.add)
                bo = nb()
                groups = []
                for hd in range(4):
                    pp = hd // 2
                    for b in range(NS):
                        j = hd * NS + b
                        groups.append((bo.v(2 * j, 2 * j + 2),
                                       [(R("S0", pp * NS * 128 + b * 128, pp * NS * 128 + (b + 1) * 128, r=True),
                                         R("qz", j, j + 2, r=True))]))
                mm_multi(groups, bo.v(0, 8 * NS))
                bo2 = Ref(bo.name, 0, 8 * NS, bo.t[:, 0:8 * NS].rearrange("p (j t) -> p j t", t=2)[:, :, 0])
                bo3 = Ref(bo.name, 0, 8 * NS, bo.t[:, 0:8 * NS].rearrange("p (a b t) -> p a b t", a=4, t=2)[:, :, :, 0])
                act(R("ofm", 0, 4 * n, r=True), bo2, AF.Copy)
                act(R("osqs", r=True), bo2, AF.Square)
                dma_out("o_glas", glas_d.rearrange("a p f -> p a f"), R("S0", 0, 2 * NS * 128, a=2))
                for j in range(2):
                    bk = nb()
                    items = []
                    for c in range(4):
                        items.append((bk.v(c * 128, (c + 1) * 128, 0, NS), R("cso", c * 2 * NS + j * NS, c * 2 * NS + (j + 1) * NS), ident()))
                    transposes(items, bk.v(0, 512), [R("cso"), ident()])
                    act(R("cstg", j * 512, (j + 1) * 512, p0=0, p1=NS, r=True), bk.v(0, 512, 0, NS), AF.Copy)
                dma_out("o_convs", convs_d[:, :], R("cstg", 0, 1024, p0=0, p1=NS))
            if is_sample:
                bk = nb()
                mm(bk.v(0, 4 * n), [(csr.v(C_OV, C_OV + 128, r=True), R("osqs", r=True))])
                t = ta()
                act(t.v(0, 4 * n), bk.v(0, 4 * n), AF.Ln, extra=[eps_ap], bias=eps_ap.ap)
                rsh = tb()
                act(rsh.v(0, 4 * n), t.v(0, 4 * n), AF.Exp, scale=-0.5)
                t2 = ta()
                stt(t2.v(0, 4 * n), R("ofm", 0, 4 * n), sm.v(0, 1), rsh.v(0, 4 * n), ALU.mult, ALU.mult)
                P.op("dve", (lambda o, i0, i1: lambda e: e.tensor_tensor(out=o.ap, in0=i0.ap, in1=i1.ap, op=ALU.mult))(
                    om.v(0, 4 * TT, r=True, a=4, sl=(0, n)), om.v(0, 4 * TT, a=4, sl=(0, n)), t2.v(0, 4 * n, a=4)),
                    reads=[om.v(0, 4 * TT), t2.v(0, 4 * n)], writes=[om.v(0, 4 * TT)])
            for m in range(8):
                if m % 2 == 0:
                    slo = wget(wo_d, m // 2)
                bk = nb()
                mm(bk.v(0, n), [(slo.v(((m % 2) * 8 + k) * 128, ((m % 2) * 8 + k + 1) * 128, r=True), om.v(k * TT, k * TT + n, r=True)) for k in range(8)])
                fo_evac(3, xn, m, bk, n, 1.0)
            post_norm(xn, n)

        def ple_p(n, t0):
            for k2 in range(2):
                dst = hgb[k2].v(0, n, r=True)
                src = p_d[k2 * 128:(k2 + 1) * 128, t0:t0 + n]
                P.op("pool", (lambda d_, s_: lambda e: e.dma_start(out=d_.ap, in_=s_))(dst, src), writes=[dst], dma="pin%d" % k2)

        def ple_stage(n, t0, next_t0=None):
            pre_norm_deferred(6, n)
            for m in range(8):
                if m % 2 == 0:
                    slg = wget(wpg_d, m // 2)
                    slp = wget(wpp_d, m // 2)
                bkg, bkp = nb(), nb()
                mm(bkg.v(0, n), [(slg.v(((m % 2) * 8 + k) * 128, ((m % 2) * 8 + k + 1) * 128, r=True), xn.v(k * TT, k * TT + n, r=True)) for k in range(8)])
                mm(bkp.v(0, n), [(slp.v(((m % 2) * 2 + k2) * 128, ((m % 2) * 2 + k2 + 1) * 128, r=True), hgb[k2].v(0, n, r=True)) for k2 in range(2)])
                t = ta()
                tt(t.v(0, n), bkg.v(0, n), rs.v(0, n), ALU.mult)
                act(t.v(0, n), t.v(0, n), AF.Sigmoid)
                if m == 7:
                    act(junk.v(0, 1, 0, 1), eps_ap_row, AF.Ln)
                tt(om.v(m * TT, m * TT + n, r=True), bkp.v(0, n), t.v(0, n), ALU.mult)
                act(sq.v(m * TT, m * TT + n, r=True), om.v(m * TT, m * TT + n), AF.Square)
            norm_stats(n, 8, C_OD)
            def m1(c):
                stt(om.v(c * TT, c * TT + n, r=True), om.v(c * TT, c * TT + n), ngl.v(7 * 8 + c, 7 * 8 + c + 1), rs.v(0, n), ALU.mult, ALU.mult)
            def ad(c):
                tt(hc(c, n), hc(c, n), om.v(c * TT, c * TT + n), ALU.add)
                dma_out("oy%d_%s" % (c, CUR["name"]), y_d[c * 128:(c + 1) * 128, t0:t0 + n], hc(c, n))
            seq = [("m", 0), ("m", 1), ("a", 0), ("m", 2), ("a", 1), ("m", 3), ("a", 2), ("m", 4), ("a", 3), ("m", 5), ("a", 4),
                   ("m", 6), ("a", 5), ("m", 7), ("a", 6), ("a", 7)]
            for k_, c in seq:
                (m1 if k_ == "m" else ad)(c)

        ntile = SEQ // TT
        SETS = [{"hcf": _hcA, "name": "A"}, {"hcf": _hcB, "name": "B"}]
        SAMPLE = {"hcf": _hcS, "name": "S"}
        CUR.update(SETS[0])
        stage_in(TT, 0)
        CUR.update(SAMPLE)
        stage_in(NS, SEQ)
        CUR.update(SETS[0])
        for ti in range(ntile):
            t0 = ti * TT
            last = ti == ntile - 1
            if not last:
                CUR.update(SETS[(ti + 1) % 2])
                stage_in(TT, t0 + TT)
            PROMPT = SETS[ti % 2]
            CUR.update(PROMPT)
            ffn(0, 0, 1, TT, samp=SampFFN(0, 0, 1) if last else None, dve_scale=(ti == 0))
            mixer(TT, False, last)
            if last:
                CUR.update(SAMPLE)
                mixer(NS, True, False)
                CUR.update(PROMPT)
            ple_p(TT, t0)
            ffn(1, 4, 5, TT, samp=SampFFN(1, 4, 5) if last else None, dve_scale=last)
            ple_stage(TT, t0)
            if last:
                CUR.update(SAMPLE)
                ple_p(NS, SEQ)
                ple_stage(NS, SEQ)
                CUR.update(PROMPT)
        assert wq["used"] == len(wq["order"])
        P.finish_waits("sp")
        P.emit(block)
    return nc


def _chunks_km(w, cols):
    K = w.shape[0] // 128
    return np.ascontiguousarray(w[:, cols].reshape(K, 128, len(cols)).transpose(1, 0, 2))


def _prep_shared(inp):
    f32 = np.float32
    sh = {}
    cst = np.zeros((128, C_W), f32)
    csr = np.zeros((128, CR_W), f32)
    cst[:, C_ID:C_ID + 128] = np.eye(128)
    csr[:, C_OD:C_OD + 128] = 1.0 / D
    csr[:, C_OV:C_OV + 128] = 1.0 / 128
    tri = np.triu(np.ones((128, 128)))
    csr[:, C_UN:C_UN + 128] = tri * (-1.0 / 16)
    csr[:, C_LN:C_LN + 128] = (1.0 - tri) * (-1.0 / 16)
    cst[:, C_MK:C_MK + 128] = tri
    csr[:, C_NI:C_NI + 128] = np.eye(128) * (-1.0 / 16)
    cst[:, C_ONE:C_ONE + 128] = 1.0
    cst[:, C_EPS] = EPS
    sh["cst"] = cst
    sh["csr"] = csr
    ng = inp["norm_g"][0]
    sh["ngl"] = np.ascontiguousarray(ng.reshape(8, 8, 128).transpose(2, 0, 1).reshape(128, 64))
    sm = np.zeros((128, 16), f32)
    sm[:, 0] = inp["gla_norm_g"][0]
    cw = inp["conv_w"][0]
    sm[:, 1:13] = cw.reshape(3, 4, 128).transpose(2, 1, 0).reshape(128, 12)
    sh["sm"] = sm
    sh["wgu"] = np.ascontiguousarray(inp["w_gate_up"][0])
    sh["bg"] = np.ascontiguousarray(inp["b_gate"][0].reshape(1, 256))
    w_in = inp["w_in"][0]
    ar = np.arange
    sh["wr"] = _chunks_km(w_in, ar(1536, 1552)).reshape(128, 128)
    for f, nm in ((1, "ffn1"), (2, "ffn2")):
        g, u, d = inp[nm + "_gate"][0], inp[nm + "_up"][0], inp[nm + "_down"][0]
        a = np.empty((NJ, 128, 2, 8, 128), f32)
        a[:, :, 0] = g.reshape(8, 128, NJ, 128).transpose(2, 1, 0, 3)
        a[:, :, 1] = u.reshape(8, 128, NJ, 128).transpose(2, 1, 0, 3)
        sh[f"f{f}a"] = a.reshape(NJ * 128, 2048)
        sh[f"f{f}b"] = np.ascontiguousarray(d.reshape(2, 11, 128, 8, 128).transpose(3, 0, 2, 1, 4)).reshape(16 * 128, 1408)
    sh["wk"] = _chunks_km(w_in, ar(256, 512)).reshape(128, 2048)
    sh["wv"] = np.concatenate([_chunks_km(w_in, ar(512 + h2 * 256, 512 + (h2 + 1) * 256)).reshape(128, 2048) for h2 in range(2)], 0)

    def pair(w, c0a, c0b):
        return np.concatenate([_chunks_km(w, ar(c0a, c0a + 128)), _chunks_km(w, ar(c0b, c0b + 128))], axis=1).reshape(128, 2048)
    CB, CC, CH = 1552, 2064, 2576
    sh["wca"] = np.concatenate([pair(w_in, CH + c * 128, CC + c * 128) for c in range(4)], 0)
    sh["wcb"] = np.concatenate([_chunks_km(w_in, ar(CB + c * 128, CB + (c + 1) * 128)).reshape(128, 1024) for c in range(4)], 0)
    sh["wkq"] = np.concatenate([pair(w_in, 256, 384), pair(w_in, 0, 128)], 0)
    sh["wg"] = np.concatenate([pair(w_in, 1024 + i * 256, 1024 + i * 256 + 128) for i in range(2)], 0)
    wo = inp["w_out"][0]
    sh["wo"] = np.concatenate([pair(wo, m2 * 256, m2 * 256 + 128) for m2 in range(4)], 0)
    wpg = inp["w_ple_gate"][0]
    sh["wpg"] = np.concatenate([pair(wpg, m2 * 256, m2 * 256 + 128) for m2 in range(4)], 0)
    wpp = inp["w_ple_proj"][0]
    sh["wpp"] = np.ascontiguousarray(wpp.reshape(2, 128, 4, 2, 128).transpose(2, 1, 3, 0, 4)).reshape(4 * 128, 512)
    return {k: np.ascontiguousarray(v, dtype=f32) for k, v in sh.items()}


_NC_CACHE = {}


def kernel(**inputs):
    inp = {k: np.asarray(v) for k, v in inputs.items()}
    shared = _prep_shared(inp)
    in_maps = []
    for c in range(NCORES):
        m = dict(shared)
        m["x"] = np.ascontiguousarray(np.concatenate([inp["x_prompt"][c], inp["x_sample"][c * NS:(c + 1) * NS, 0]], 0).T, dtype=np.float32)
        m["p"] = np.ascontiguousarray(np.concatenate([inp["p_prompt"][0, c], inp["p_sample"][0, c * NS:(c + 1) * NS, 0]], 0).T, dtype=np.float32)
        sg = inp["state_gla"][0, c * NS:(c + 1) * NS]
        m["sg"] = np.ascontiguousarray(sg.reshape(NS, 2, 128, 128).transpose(2, 1, 0, 3).reshape(128, 2 * NS * 128), dtype=np.float32)
        sc = inp["state_conv"][0, c * NS:(c + 1) * NS]
        m["sc"] = np.ascontiguousarray(sc.reshape(NS, 2, 4, 128).transpose(3, 2, 1, 0).reshape(128, 4 * 2 * NS), dtype=np.float32)
        in_maps.append(m)
    if "nc" not in _NC_CACHE:
        _NC_CACHE["nc"] = build_program()
    res = run_bass_kernel_spmd(_NC_CACHE["nc"], in_maps, core_ids=list(range(NCORES)))
    R = res.results
    y_p = np.stack([np.ascontiguousarray(R[c]["y"][:, :SEQ].T) for c in range(NCORES)], 0).astype(np.float32)
    y_s = np.concatenate([R[c]["y"][:, SEQ:].T for c in range(NCORES)], 0).reshape(NCORES * NS, 1, D).astype(np.float32)
    gla_p = np.stack([R[c]["gla_p"].reshape(4, 64, 128) for c in range(NCORES)], 0)[None].astype(np.float32)
    conv_p = np.stack([R[c]["conv_p"] for c in range(NCORES)], 0)[None].astype(np.float32)
    gla_s = np.concatenate([R[c]["gla_s"].reshape(2, 128, NS, 128).transpose(2, 0, 1, 3).reshape(NS, 4, 64, 128)
                            for c in range(NCORES)], 0)[None].astype(np.float32)
    conv_s = np.concatenate([R[c]["conv_s"].reshape(NS, 2, 512) for c in range(NCORES)], 0)[None].astype(np.float32)
    return (y_p, y_s, gla_p, conv_p, gla_s, conv_s)
```

```python
import contextlib
import numpy as np
import concourse.bass as bass
import concourse.mybir as mybir
from concourse.bass_utils import run_bass_kernel_spmd

F32 = mybir.dt.float32
F32R = mybir.dt.float32r
AF = mybir.ActivationFunctionType
ALU = mybir.AluOpType

NCORES = 8
D = 1024
SEQ = 2048
NS = 16
DFF = 2816
NJ = DFF // 128
PLE = 256
EPS = 1e-6
TT = 512
NTOK = SEQ + NS
SLOT_W = 2048
NSLOT = 5
COMPUTE = ("pe", "act", "dve", "pool")

C_ID, C_MK, C_ONE, C_EPS = 0, 128, 256, 384
C_W = 392
C_OD, C_OV, C_UN, C_LN, C_NI = 0, 128, 256, 384, 512
CR_W = 640


class Ref:
    __slots__ = ("base", "lo", "hi", "ap")

    def __init__(self, base, lo, hi, ap):
        self.base, self.lo, self.hi, self.ap = base, lo, hi, ap


class Buf:
    def __init__(self, name, t, P, W):
        self.name, self.t, self.P, self.W = name, t, P, W

    def v(self, lo, hi, p0=0, p1=None, r=False, a=None, sl=None):
        p1 = self.P if p1 is None else p1
        ap = self.t[p0:p1, lo:hi]
        if r:
            ap = ap.bitcast(F32R)
        if a is not None:
            ap = ap.rearrange("p (a b) -> p a b", a=a)
            if sl is not None:
                ap = ap[:, :, sl[0]:sl[1]]
        return Ref(self.name, lo, hi, ap)


class Prog:
    def __init__(self, nc, stack):
        self.nc, self.stack = nc, stack
        self.streams = {e: [] for e in ("pe", "act", "dve", "pool", "sp")}
        self.seq = {e: 0 for e in COMPUTE}
        self.esem = {e: stack.enter_context(nc.semaphore("s_" + e)) for e in COMPUTE}
        self.dsem, self.dcount = {}, {}
        self.known = {e: {} for e in self.streams}
        self.recs = {}
        self.out_tokens = []
        self.psum_names = set()

    def sbuf(self, name, P, W):
        t = self.stack.enter_context(self.nc.sbuf_tensor("sb_" + name, [P, W], F32))
        return Buf(name, t, P, W)

    def psum(self, name):
        t = self.stack.enter_context(self.nc.psum_tensor("ps_" + name, [128, 512], F32))
        self.psum_names.add(name)
        return Buf(name, t, 128, 512)

    def _deps_read(self, r, deps, eng):
        ps = r.base in self.psum_names
        for rec in self.recs.get(r.base, []):
            if ps or (rec[0] < r.hi and r.lo < rec[1]):
                if rec[2] is not None:
                    deps.add(rec[2])
                if ps:
                    for k_, v_ in rec[3].items():
                        if not (k_[0] == "e" and k_[1] == eng):
                            deps.add((k_[0], k_[1], v_))

    def _deps_write(self, w, deps):
        ps = w.base in self.psum_names
        for rec in self.recs.get(w.base, []):
            if ps or (rec[0] < w.hi and w.lo < rec[1]):
                if rec[2] is not None:
                    deps.add(rec[2])
                for k_, v_ in rec[3].items():
                    deps.add((k_[0], k_[1], v_))

    def _reg_read(self, r, tok):
        covered = False
        for rec in self.recs.setdefault(r.base, []):
            if rec[0] < r.hi and r.lo < rec[1]:
                k_ = (tok[0], tok[1])
                rec[3][k_] = max(rec[3].get(k_, 0), tok[2])
                covered = True
        if not covered:
            self.recs[r.base].append([r.lo, r.hi, None, {(tok[0], tok[1]): tok[2]}])

    def _reg_write(self, w, tok):
        lst = self.recs.setdefault(w.base, [])
        lst[:] = [rec for rec in lst if not (w.lo <= rec[0] and rec[1] <= w.hi)]
        lst.append([w.lo, w.hi, tok, {}])

    def op(self, eng, fn, reads=(), writes=(), dma=None, ndma=1, is_output=False):
        deps = set()
        for r in reads:
            self._deps_read(r, deps, eng)
        for w in writes:
            self._deps_write(w, deps)
        if dma is None:
            self.seq[eng] += 1
            tok = ("e", eng, self.seq[eng])
        else:
            if dma not in self.dsem:
                self.dsem[dma] = self.stack.enter_context(self.nc.semaphore("d_" + dma))
                self.dcount[dma] = 0
            self.dcount[dma] += 16 * ndma
            tok = ("d", dma, self.dcount[dma])
        need = {}
        for d in deps:
            key = (d[0], d[1])
            if d[0] == "e" and d[1] == eng and eng == "pe":
                continue
            if self.known[eng].get(key, 0) >= d[2]:
                continue
            need[key] = max(need.get(key, 0), d[2])
        for key, val in need.items():
            self.known[eng][key] = val
        self.streams[eng].append((sorted(need.items()), fn, tok, ndma))
        for r in reads:
            self._reg_read(r, tok)
        for w in writes:
            self._reg_write(w, tok)
        if is_output:
            self.out_tokens.append(tok)
        return tok

    def finish_waits(self, eng="sp"):
        need = {}
        for d in self.out_tokens:
            key = (d[0], d[1])
            need[key] = max(need.get(key, 0), d[2])
        self.streams[eng].append((sorted(need.items()), None, None, 0))

    def emit(self, block):
        prog = self

        def sem(key):
            return prog.esem[key[1]] if key[0] == "e" else prog.dsem[key[1]]

        def run(engname, e):
            for waits, fn, tok, ndma in prog.streams[engname]:
                for key, val in waits:
                    e.wait_ge(sem(key), val)
                if fn is None:
                    continue
                res = fn(e)
                if tok[0] == "e":
                    ins = res[-1] if isinstance(res, (list, tuple)) else res
                    ins.then_inc(prog.esem[tok[1]], 1)
                else:
                    lst = res if isinstance(res, (list, tuple)) else [res]
                    assert len(lst) == ndma
                    for ins in lst:
                        ins.then_inc(prog.dsem[tok[1]], 16)

        @block.tensor
        def _(e):
            run("pe", e)

        @block.scalar
        def _(e):
            run("act", e)

        @block.vector
        def _(e):
            run("dve", e)

        @block.gpsimd
        def _(e):
            run("pool", e)

        @block.sync
        def _(e):
            run("sp", e)


def build_program():
    nc = bass.Bass("TRN2", target_bir_lowering=False)

    def din(name, shape):
        return nc.dram_tensor(name, list(shape), F32, kind="ExternalInput").ap()

    def dout(name, shape):
        return nc.dram_tensor(name, list(shape), F32, kind="ExternalOutput").ap()

    x_d = din("x", (D, NTOK))
    p_d = din("p", (PLE, NTOK))
    sg_d = din("sg", (128, 2 * NS * 128))
    sc_d = din("sc", (128, 4 * 2 * NS))
    cst_d = din("cst", (128, C_W))
    csr_d = din("csr", (128, CR_W))
    ngl_d = din("ngl", (128, 64))
    sm_d = din("sm", (128, 16))
    wgu_d = din("wgu", (16, 256))
    bg_d = din("bg", (1, 256))
    wr_d = din("wr", (128, 128))
    fA_d = [din(f"f{f}a", (NJ * 128, 2048)) for f in (1, 2)]
    fB_d = [din(f"f{f}b", (16 * 128, 1408)) for f in (1, 2)]
    wk_d = din("wk", (128, 2048))
    wv_d = din("wv", (2 * 128, 2048))
    wca_d = din("wca", (4 * 128, 2048))
    wcb_d = din("wcb", (4 * 128, 1024))
    wkq_d = din("wkq", (2 * 128, 2048))
    wg_d = din("wg", (2 * 128, 2048))
    wo_d = din("wo", (4 * 128, 2048))
    wpp_d = din("wpp", (4 * 128, 512))
    wpg_d = din("wpg", (4 * 128, 2048))

    y_d = dout("y", (D, NTOK))
    glap_d = dout("gla_p", (2, 128, 128))
    convp_d = dout("conv_p", (2, 512))
    glas_d = dout("gla_s", (2, 128, NS * 128))
    convs_d = dout("conv_s", (NS, 1024))

    with contextlib.ExitStack() as stack:
        P = Prog(nc, stack)
        h = P.sbuf("h", 128, 8 * TT)
        xn = P.sbuf("xn", 128, 8 * TT)
        sq = P.sbuf("sq", 128, 8 * TT)
        om = P.sbuf("om", 128, 8 * TT)
        big = P.sbuf("big", 128, 13312)
        stgs = [P.sbuf(f"stg{i}", 128, D) for i in range(4)]
        qtz = P.sbuf("qtz", 128, 4 * TT)
        cst = P.sbuf("cst", 128, C_W)
        csr = P.sbuf("csr", 128, CR_W)
        ngl = P.sbuf("ngl", 128, 64)
        sm = P.sbuf("sm", 128, 16)
        nglh = P.sbuf("nglh", 128, 64)
        wgu = P.sbuf("wgu", 16, 256)
        bg = P.sbuf("bg", 1, 256)
        wrs = P.sbuf("wrs", 128, 128)
        Sst = P.sbuf("Sst", 128, 256)
        Sr = [P.sbuf(f"Sr{i}", 128, 256) for i in range(2)]
        carry = P.sbuf("carry", 128, 8)
        rs = P.sbuf("rs", 128, TT)
        junk = P.sbuf("junk", 128, 8)
        hgb = [P.sbuf(f"hgb{i}", 128, TT) for i in range(2)]
        hs = P.sbuf("hs", 128, 8 * NS)
        rss = P.sbuf("rss", 128, 2 * NS)
        tmpa = [P.sbuf(f"tmpa{i}", 128, TT) for i in range(2)]
        tmpb = [P.sbuf(f"tmpb{i}", 128, TT) for i in range(2)]
        slots = [P.sbuf(f"wslot{i}", 128, SLOT_W) for i in range(NSLOT)]
        banks = [P.psum(f"bank{i}") for i in range(8)]
        block = stack.enter_context(nc.Block())

        st_ = {"bank": 0, "ta": 0, "tb": 0, "stg": 0}

        def nb():
            b = banks[st_["bank"] % 8]
            st_["bank"] += 1
            return b

        def ta():
            t = tmpa[st_["ta"] % 2]
            st_["ta"] += 1
            return t

        def tb():
            t = tmpb[st_["tb"] % 2]
            st_["tb"] += 1
            return t

        def dma_in(eng, name, dst, src, is_output=False):
            P.op(eng, lambda e: e.dma_start(out=dst.ap, in_=src), writes=[dst], dma=name)

        def dma_out(name, dst_ap, src):
            P.op("sp", lambda e: e.dma_start(out=dst_ap, in_=src.ap), reads=[src], dma=name, is_output=True)

        def mm(out, pairs, extra=(), first=True, last=True):
            def fn(e):
                n_ = len(pairs)
                for i, (a, b) in enumerate(pairs):
                    ins = e.matmul(out.ap, a.ap, b.ap, start=(first and i == 0), stop=(last and i == n_ - 1))
                return ins
            rd = []
            for a, b in pairs:
                rd.append(a)
                rd.append(b)
            P.op("pe", fn, reads=rd + list(extra), writes=[out])

        def mm_multi(groups, out_cover):
            def fn(e):
                for out, pairs in groups:
                    n_ = len(pairs)
                    for i, (a, b) in enumerate(pairs):
                        ins = e.matmul(out.ap, a.ap, b.ap, start=(i == 0), stop=(i == n_ - 1))
                return ins
            rd = []
            for out, pairs in groups:
                for a, b in pairs:
                    rd.append(a)
                    rd.append(b)
            P.op("pe", fn, reads=rd, writes=[out_cover])

        def transposes(items, out_cover, extra_reads):
            def fn(e):
                for out, in_, idn in items:
                    ins = e.transpose(out.ap, in_.ap, idn.ap)
                return ins
            rd = list(extra_reads)
            P.op("pe", fn, reads=rd, writes=[out_cover])

        def act(out, in_, func, extra=(), **kw):
            P.op("act", lambda e: e.activation(out=out.ap, in_=in_.ap, func=func, **kw), reads=[in_] + list(extra), writes=[out])

        def tt(out, in0, in1, op, eng="dve"):
            P.op(eng, lambda e: e.tensor_tensor(out=out.ap, in0=in0.ap, in1=in1.ap, op=op), reads=[in0, in1], writes=[out])

        def stt(out, in0, scalar, in1, op0, op1):
            sc = scalar.ap if isinstance(scalar, Ref) else scalar
            rd = [in0, in1] + ([scalar] if isinstance(scalar, Ref) else [])
            P.op("dve", lambda e: e.scalar_tensor_tensor(out=out.ap, in0=in0.ap, scalar=sc, in1=in1.ap, op0=op0, op1=op1),
                 reads=rd, writes=[out])

        def memset(buf_ref, val=0.0):
            P.op("dve", lambda e: e.memset(buf_ref.ap, val), writes=[buf_ref])

        wq = {"order": [], "issued": 0, "used": 0}

        def weight_order():
            def ffn_w(f):
                return [(fA_d[f], j, 2048) for j in range(NJ)] + [(fB_d[f], mh, 1408) for mh in range(16)]

            def mix_w():
                o = [(wk_d, 0, 2048), (wv_d, 0, 2048), (wv_d, 1, 2048), (wkq_d, 0, 2048), (wkq_d, 1, 2048), (wg_d, 0, 2048), (wg_d, 1, 2048)]
                for c in range(4):
                    o += [(wca_d, c, 2048), (wcb_d, c, 1024)]
                o += [(wo_d, m2, 2048) for m2 in range(4)]
                return o

            def ple_w():
                o = []
                for m2 in range(4):
                    o += [(wpg_d, m2, 2048), (wpp_d, m2, 512)]
                return o
            o = []
            for ti in range(SEQ // TT):
                last = ti == SEQ // TT - 1
                o += ffn_w(0) + mix_w() + (mix_w() if last else []) + ffn_w(1) + ple_w() + (ple_w() if last else [])
            return o

        wq["order"] = weight_order()

        def wissue():
            i = wq["issued"]
            d, blk, words = wq["order"][i]
            sl = slots[i % NSLOT]
            dst = sl.v(0, words, r=True)
            src = d[blk * 128:(blk + 1) * 128, :]
            P.op("pool", lambda e: e.dma_start(out=dst.ap, in_=src), writes=[dst], dma=f"ws{i % NSLOT}")
            wq["issued"] += 1

        def wget(d, blk, cap=None):
            i = wq["used"]
            assert wq["order"][i][0] is d and wq["order"][i][1] == blk, (i, blk)
            lim = i + NSLOT - 1
            if cap is not None:
                lim = min(lim, cap + 1)
            while wq["issued"] < min(len(wq["order"]), lim):
                wissue()
            wq["used"] += 1
            return slots[i % NSLOT]

        dma_in("sp", "c_cst", cst.v(0, C_W), cst_d[:, :])
        dma_in("sp", "c_ngl", ngl.v(0, 64), ngl_d[:, :])
        dma_in("sp", "c_sm", sm.v(0, 16), sm_d[:, :])
        dma_in("sp", "c_bg", bg.v(0, 256), bg_d[:, :])
        P.op("pool", lambda e: e.dma_start(out=csr.v(0, CR_W, r=True).ap, in_=csr_d[:, :]), writes=[csr.v(0, CR_W)], dma="c_csr")
        P.op("pool", lambda e: e.dma_start(out=wgu.v(0, 256, r=True).ap, in_=wgu_d[:, :]), writes=[wgu.v(0, 256)], dma="c_wgu")
        P.op("pool", lambda e: e.dma_start(out=wrs.v(0, 128, r=True).ap, in_=wr_d[:, :]), writes=[wrs.v(0, 128)], dma="c_wr")
        ZW = 256
        zt = P.sbuf("zt", 128, ZW)
        memset(zt.v(0, ZW))

        def zero_r(ref_fn, lo, hi):
            for a_ in range(lo, hi, ZW):
                b_ = min(hi, a_ + ZW)
                o_ = ref_fn(a_, b_)
                P.op("dve", (lambda o, i: lambda e: e.tensor_copy(out=o.ap, in_=i.ap))(o_, zt.v(0, b_ - a_)),
                     reads=[zt.v(0, b_ - a_)], writes=[o_])
        zero_r(lambda a_, b_: qtz.v(a_, b_, r=True), 0, 4 * TT)
        memset(Sst.v(0, 256))
        zero_r(lambda a_, b_: Sr[0].v(a_, b_, r=True), 0, 256)
        memset(carry.v(0, 8))
        P.op("dve", lambda e: e.tensor_scalar(out=nglh.v(0, 64).ap, in0=ngl.v(0, 64).ap, scalar1=0.5, scalar2=0.0, op0=ALU.mult, op1=ALU.add),
             reads=[ngl.v(0, 64)], writes=[nglh.v(0, 64)])
        ident = lambda k=128: cst.v(C_ID, C_ID + k, 0, k)
        eps_ap = cst.v(C_EPS, C_EPS + 1)
        eps_ap_row = cst.v(C_EPS, C_EPS + 1, 0, 1)

        def nstg():
            b_ = stgs[2 + st_["stg"] % 2]
            st_["stg"] += 1
            return b_

        xq = {"n": 0, "pending": {}}

        def x_issue(t0, n, s):
            rows = min(n, 128)
            stg = stgs[xq["n"] % 2]
            xq["n"] += 1
            dst = stg.v(0, D, 0, rows)
            src = x_d[t0 + s * rows:t0 + (s + 1) * rows, :]
            P.op("sp", (lambda d_, s_: lambda e: e.dma_start(out=d_.ap, in_=s_))(dst, src), writes=[dst], dma="in_" + stg.name)
            xq["pending"][(t0, s)] = stg

        def _hcA(c, n):
            return h.v(c * TT, c * TT + n)

        def _hcB(c, n):
            return stgs[c // 2].v((c % 2) * 512, (c % 2) * 512 + n)

        def _hcS(c, n):
            return hs.v(c * NS, c * NS + n)

        CUR = {"hcf": _hcA, "name": "A"}

        def hc(c, n):
            return CUR["hcf"](c, n)

        def norm_stats(n, nch, ones_col):
            bk = nb()
            mm(bk.v(0, n), [(csr.v(ones_col, ones_col + 128, r=True), sq.v(c * TT, c * TT + n, r=True)) for c in range(nch)])
            t = ta()
            act(t.v(0, n), bk.v(0, n), AF.Ln, extra=[eps_ap], bias=eps_ap.ap)
            act(rs.v(0, n), t.v(0, n), AF.Exp, scale=-0.5)

        def ew(eng, kind, **kw):
            if kind == "stt":
                out, in0, scalar, in1, op0, op1 = kw["out"], kw["in0"], kw["scalar"], kw["in1"], kw["op0"], kw["op1"]
                sc = scalar.ap if isinstance(scalar, Ref) else scalar
                rd = [in0, in1] + ([scalar] if isinstance(scalar, Ref) else [])
                P.op(eng, lambda e: e.scalar_tensor_tensor(out=out.ap, in0=in0.ap, scalar=sc, in1=in1.ap, op0=op0, op1=op1),
                     reads=rd, writes=[out])
            else:
                out, in0, in1, op = kw["out"], kw["in0"], kw["in1"], kw["op"]
                P.op(eng, lambda e: e.tensor_tensor(out=out.ap, in0=in0.ap, in1=in1.ap, op=op), reads=[in0, in1], writes=[out])

        NDVE = 8

        def pre_norm(g, n):
            for c in range(8):
                act(sq.v(c * TT, c * TT + n, r=True), hc(c, n), AF.Square)
            for c in range(NDVE, 8):
                P.op("pool", (lambda o, i, w: lambda e: e.tensor_scalar(out=o.ap, in0=i.ap, scalar1=w.ap, scalar2=1.0, op0=ALU.mult, op1=ALU.mult))(
                    xn.v(c * TT, c * TT + n, r=True), hc(c, n), ngl.v(g * 8 + c, g * 8 + c + 1)),
                    reads=[hc(c, n), ngl.v(g * 8 + c, g * 8 + c + 1)], writes=[xn.v(c * TT, c * TT + n)])
            norm_stats(n, 8, C_OD)
            for c in range(NDVE):
                ew("dve", "stt", out=xn.v(c * TT, c * TT + n, r=True), in0=hc(c, n),
                   scalar=ngl.v(g * 8 + c, g * 8 + c + 1), in1=rs.v(0, n), op0=ALU.mult, op1=ALU.mult)
            for c in range(NDVE, 8):
                ew("pool", "tt", out=xn.v(c * TT, c * TT + n, r=True), in0=xn.v(c * TT, c * TT + n), in1=rs.v(0, n), op=ALU.mult)

        def pre_norm_deferred(g, n, stats=True, done=False, dve_scale=False):
            for c in (() if done else range(8)):
                act(sq.v(c * TT, c * TT + n, r=True), hc(c, n), AF.Square)
                gcol = ngl.v(g * 8 + c, g * 8 + c + 1)
                if dve_scale:
                    P.op("dve", (lambda o, i, w: lambda e: e.tensor_scalar(out=o.ap, in0=i.ap, scalar1=w.ap, scalar2=0.0, op0=ALU.mult, op1=ALU.add))(
                        xn.v(c * TT, c * TT + n, r=True), hc(c, n), gcol), reads=[hc(c, n), gcol], writes=[xn.v(c * TT, c * TT + n)])
                else:
                    P.op("act", (lambda o, i, w: lambda e: e.activation(out=o.ap, in_=i.ap, func=AF.Identity, scale=w.ap))(
                        xn.v(c * TT, c * TT + n, r=True), hc(c, n), gcol), reads=[hc(c, n), gcol], writes=[xn.v(c * TT, c * TT + n)])
            if stats:
                norm_stats(n, 8, C_OD)

        def fo_evac(g, fo, m, bk, n, scale, col0=0):
            gcol = (nglh if scale == 0.5 else ngl).v(g * 8 + m, g * 8 + m + 1)
            P.op("act", (lambda o, i, w: lambda e: e.activation(out=o.ap, in_=i.ap, func=AF.Identity, scale=w.ap))(
                fo.v(m * TT, m * TT + n, r=True), bk.v(col0, col0 + n), gcol),
                reads=[bk.v(col0, col0 + n), gcol], writes=[fo.v(m * TT, m * TT + n)])
            act(sq.v(m * TT, m * TT + n, r=True), bk.v(col0, col0 + n), AF.Square)

        def post_norm(fo, n):
            norm_stats(n, 8, C_OD)
            def m1(eng, c):
                ew(eng, "tt", out=fo.v(c * TT, c * TT + n, r=True), in0=fo.v(c * TT, c * TT + n), in1=rs.v(0, n), op=ALU.mult)
            def ad(eng, c):
                ew(eng, "tt", out=hc(c, n), in0=hc(c, n), in1=fo.v(c * TT, c * TT + n), op=ALU.add)
            seq_d = [("m", 0), ("m", 1), ("a", 0), ("m", 2), ("a", 1), ("m", 3), ("a", 2), ("m", 4), ("a", 3), ("m", 5), ("a", 4),
                     ("m", 6), ("a", 5), ("m", 7), ("a", 6), ("a", 7)]
            seq_p = []
            ip = 0
            for i, (k, c) in enumerate(seq_d):
                (m1 if k == "m" else ad)("dve", c)
                if i % 2 == 1 and ip < len(seq_p):
                    k2, c2 = seq_p[ip]; ip += 1
                    (m1 if k2 == "m" else ad)("pool", c2)
            while ip < len(seq_p):
                k2, c2 = seq_p[ip]; ip += 1
                (m1 if k2 == "m" else ad)("pool", c2)

        def x_chunk(c, n, t0):
            dst = hc(c, n)
            src = x_d[c * 128:(c + 1) * 128, t0:t0 + n]
            P.op("sp", (lambda d_, s_: lambda e: e.dma_start(out=d_.ap, in_=s_))(dst, src), writes=[dst], dma="xin%d_%s" % (c, CUR["name"]))

        def stage_in(n, t0):
            for c in range(8):
                x_chunk(c, n, t0)

        class SampFFN:
            XS, SQS, HF = 11264, 11264 + 128, 11264 + 256
            HT, FT = 0, 2816

            def __init__(self, f, gpre, gpost):
                self.f, self.gpre, self.gpost = f, gpre, gpost
                self.rsc = junk.v(4, 5, 0, NS)

            def xs(self, k, r=True):
                return big.v(self.XS + k * NS, self.XS + (k + 1) * NS, r=r)

            def sqs(self, k, r=True):
                return big.v(self.SQS + k * NS, self.SQS + (k + 1) * NS, r=r)

            def hsc(self, c):
                return hs.v(c * NS, (c + 1) * NS)

            def stats(self):
                bk = nb()
                mm(bk.v(0, NS), [(csr.v(C_OD, C_OD + 128, r=True), self.sqs(c)) for c in range(8)])
                act(rss.v(NS, 2 * NS), bk.v(0, NS), AF.Ln, extra=[eps_ap], bias=eps_ap.ap)
                act(rss.v(0, NS), rss.v(NS, 2 * NS), AF.Exp, scale=-0.5)

            def pre(self):
                n = NS
                for c in range(8):
                    act(self.sqs(c), self.hsc(c), AF.Square)
                    gcol = ngl.v(self.gpre * 8 + c, self.gpre * 8 + c + 1)
                    P.op("act", (lambda o, i, w: lambda e: e.activation(out=o.ap, in_=i.ap, func=AF.Identity, scale=w.ap))(
                        self.xs(c), self.hsc(c), gcol), reads=[self.hsc(c), gcol], writes=[self.xs(c)])
                self.stats()
                bk = nb()
                mm(bk.v(0, 1, 0, n), [(rss.v(0, n, 0, 1), cst.v(C_ONE, C_ONE + 1, 0, 1))])
                act(self.rsc, bk.v(0, 1, 0, n), AF.Copy)

            def chunk_a(self, j, sl):
                n = NS
                bk = nb()
                groups = [(bk.v(u * 128, (u + 1) * 128, 0, n),
                           [(self.xs(k), sl.v(u * 1024 + k * 128, u * 1024 + (k + 1) * 128, r=True)) for k in range(8)])
                          for u in range(2)]
                mm_multi(groups, bk.v(0, 256))
                t = ta()
                rsc = self.rsc
                P.op("act", (lambda o, i, w: lambda e: e.activation(out=o.ap, in_=i.ap, func=AF.Silu, scale=w.ap))(
                    t.v(0, 128, 0, n), bk.v(0, 128, 0, n), rsc), reads=[bk.v(0, 128, 0, n), rsc], writes=[t.v(0, 128, 0, n)])
                stt(om.v(self.HT + j * 128, self.HT + (j + 1) * 128, 0, n, r=True), bk.v(128, 256, 0, n), rsc, t.v(0, 128, 0, n), ALU.mult, ALU.mult)

            def post_a(self):
                n = NS
                for g4 in range(0, NJ, 4):
                    js = list(range(g4, min(NJ, g4 + 4)))
                    bk = nb()
                    items = [(bk.v(jj * n, (jj + 1) * n), om.v(self.HT + j * 128, self.HT + (j + 1) * 128, 0, n), ident(n)) for jj, j in enumerate(js)]
                    transposes(items, bk.v(0, len(js) * n), [om.v(self.HT + g4 * 128, self.HT + (g4 + len(js)) * 128), ident()])
                    act(big.v(self.HF + g4 * n, self.HF + (g4 + len(js)) * n, r=True), bk.v(0, len(js) * n), AF.Copy)

            def chunk_b(self, m, half, sl):
                n = NS
                if half == 0:
                    self.bkb = nb()
                bk = self.bkb
                mm(bk.v(0, 128, 0, n), [(big.v(self.HF + (half * 11 + kk) * n, self.HF + (half * 11 + kk + 1) * n, r=True),
                                         sl.v(kk * 128, (kk + 1) * 128, r=True)) for kk in range(11)], first=(half == 0), last=(half == 1))
                if half == 1:
                    act(om.v(self.FT + m * 128, self.FT + (m + 1) * 128, 0, n, r=True), bk.v(0, 128, 0, n), AF.Copy)

            def post_b(self):
                n = NS
                bk = nb()
                items = [(bk.v(m * n, (m + 1) * n), om.v(self.FT + m * 128, self.FT + (m + 1) * 128, 0, n), ident(n)) for m in range(8)]
                transposes(items, bk.v(0, 8 * n), [om.v(self.FT, self.FT + 1024), ident()])
                for m in range(8):
                    gcol = nglh.v(self.gpost * 8 + m, self.gpost * 8 + m + 1)
                    P.op("act", (lambda o, i, w: lambda e: e.activation(out=o.ap, in_=i.ap, func=AF.Identity, scale=w.ap))(
                        self.xs(m), bk.v(m * n, (m + 1) * n), gcol), reads=[bk.v(m * n, (m + 1) * n), gcol], writes=[self.xs(m)])
                    act(self.sqs(m), bk.v(m * n, (m + 1) * n), AF.Square)
                self.stats()
                for c in range(8):
                    tt(self.xs(c), self.xs(c, r=False), rss.v(0, n), ALU.mult)
                    tt(self.hsc(c), self.hsc(c), self.xs(c, r=False), ALU.add)

        def ffn(f, gpre, gpost, n, samp=None, dve_scale=False):
            pre_norm_deferred(gpre, n, stats=False, dve_scale=dve_scale)
            if samp is not None:
                samp.pre()
            hid = lambda j: big.v(j * TT, j * TT + n, r=True)
            NW = 3
            i0 = wq["used"]
            wsl = [wget(fA_d[f], j, cap=i0 + NSLOT - 1) for j in range(NW)]
            wbk = [(nb(), nb()) for j in range(NW)]
            for k in range(8):
                def fnk(e, k=k):
                    for j in range(NW):
                        for u in range(2):
                            ins = e.matmul(wbk[j][u].v(0, n).ap, wsl[j].v(u * 1024 + k * 128, u * 1024 + (k + 1) * 128, r=True).ap,
                                           xn.v(k * TT, k * TT + n, r=True).ap, start=(k == 0), stop=(k == 7))
                    return ins
                P.op("pe", fnk, reads=[xn.v(k * TT, k * TT + n)] + [wsl[j].v(0, 2048) for j in range(NW)],
                     writes=[wbk[j][u].v(0, n) for j in range(NW) for u in range(2)])
            norm_stats(n, 8, C_OD)
            for j in range(NJ):
                if j < NW:
                    bg_, bu_ = wbk[j]
                else:
                    sl = wget(fA_d[f], j)
                    bg_, bu_ = nb(), nb()
                    mm(bg_.v(0, n), [(sl.v(k * 128, (k + 1) * 128, r=True), xn.v(k * TT, k * TT + n, r=True)) for k in range(8)])
                    mm(bu_.v(0, n), [(sl.v(1024 + k * 128, 1024 + (k + 1) * 128, r=True), xn.v(k * TT, k * TT + n, r=True)) for k in range(8)])
                t, t2 = ta(), tb()
                tt(t.v(0, n), bg_.v(0, n), rs.v(0, n), ALU.mult)
                act(t.v(0, n), t.v(0, n), AF.Silu)
                tt(t2.v(0, n), bu_.v(0, n), rs.v(0, n), ALU.mult)
                tt(hid(j), t2.v(0, n), t.v(0, n), ALU.mult)
                if samp is not None:
                    samp.chunk_a(j, wsl[j] if j < NW else sl)
            if samp is not None:
                samp.post_a()
            act(junk.v(0, 1, 0, 1), eps_ap_row, AF.Ln)
            for m in range(8):
                bk = nb()
                for half in range(2):
                    sl = wget(fB_d[f], 2 * m + half)
                    mm(bk.v(0, n), [(sl.v(kk * 128, (kk + 1) * 128, r=True), hid(half * 11 + kk)) for kk in range(11)],
                       first=(half == 0), last=(half == 1))
                    if samp is not None:
                        samp.chunk_b(m, half, sl)
                fo_evac(gpost, xn, m, bk, n, 0.5)
            post_norm(xn, n)
            if samp is not None:
                samp.post_b()

        def carve(n):
            o = {}
            off = 0
            def take(name, w):
                nonlocal off
                o[name] = (off, off + w)
                off += w
            take("r", max(n, 16))
            take("sp", 4 * 256 if n == TT else 256)
            take("E", 2 * n)
            take("Einv", 2 * n)
            take("kt", 2 * n)
            take("vtm", 4 * 512)
            take("kdec", 4 * 256)
            take("ucx0", n + 2)
            take("ucx1", n + 2)
            if n == TT:
                take("attm0", 512)
                take("attm1", 512)
                take("osq0", 512)
                take("osq1", 512)
            take("ofm", 4 * n)
            if n == NS:
                take("S0", 2 * NS * 128)
                take("vsel0", NS * 128)
                take("vsel1", NS * 128)
                take("qz", 4 * NS + 2)
                take("osqs", 4 * NS)
                take("cso", 4 * 2 * NS)
                take("sc", 4 * 2 * NS)
                take("cstg", 1024)
            assert off <= 13312, off
            return o

        def mixer(n, is_sample, last_prompt):
            L = carve(n)
            R = lambda name, lo=0, hi=None, **kw: big.v(L[name][0] + lo, L[name][0] + (hi if hi is not None else L[name][1] - L[name][0]), **kw)
            nsub = max(1, n // 128)
            rows = min(n, 128)
            xk = lambda k, r=True: xn.v(k * TT, k * TT + n, r=r)
            if is_sample:
                dma_in("pool", "sg", R("S0", r=True), sg_d[:, :])
                dma_in("pool", "scin", R("sc", r=True), sc_d[:, :])
            bk_r = nb()
            for c in range(8):
                act(sq.v(c * TT, c * TT + n, r=True), hc(c, n), AF.Square)
                gcol = ngl.v(2 * 8 + c, 2 * 8 + c + 1)
                hg = hgb[c % 2].v(0, n, r=True)
                P.op("act", (lambda o, i, w: lambda e: e.activation(out=o.ap, in_=i.ap, func=AF.Identity, scale=w.ap))(hg, hc(c, n), gcol),
                     reads=[hc(c, n), gcol], writes=[hg])
                mm(bk_r.v(0, n, 0, 16), [(wrs.v(c * 16, (c + 1) * 16, r=True), hg)], first=(c == 0), last=(c == 7))
            norm_stats(n, 8, C_OD)
            tt(R("r", 0, n, p0=0, p1=16, r=True), bk_r.v(0, n, 0, 16), rs.v(0, n, 0, 16), ALU.mult)
            for c in range(8):
                ew("dve", "stt", out=xn.v(c * TT, c * TT + n, r=True), in0=hc(c, n),
                   scalar=ngl.v(2 * 8 + c, 2 * 8 + c + 1), in1=rs.v(0, n), op0=ALU.mult, op1=ALU.mult)
            for s in range(nsub):
                bk = nb()
                mm(bk.v(0, 256, 0, rows), [(R("r", s * rows, (s + 1) * rows, p0=0, p1=16, r=True), wgu.v(0, 256, r=True)),
                                            (cst.v(C_ONE, C_ONE + rows, 0, 1), bg.v(0, 256))])
                t = ta()
                act(t.v(0, 256, 0, rows), bk.v(0, 256, 0, rows), AF.Exp, scale=-1.0)
                act(R("sp", s * 256, (s + 1) * 256, p0=0, p1=rows, r=True), t.v(0, 256, 0, rows), AF.Ln, bias=1.0)
            for pp in range(2):
                bk = nb()
                groups = []
                for s in range(nsub):
                    lhsT = R("sp", s * 256 + pp * 128, s * 256 + (pp + 1) * 128, p0=0, p1=rows, r=True)
                    rhs = csr.v(C_UN, C_UN + rows, 0, rows, r=True) if not is_sample else csr.v(C_NI, C_NI + rows, 0, rows, r=True)
                    groups.append((bk.v(s * rows, (s + 1) * rows), [(lhsT, rhs)]))
                mm_multi(groups, bk.v(0, n))
                act(R("E", pp * n, (pp + 1) * n, r=True), bk.v(0, n), AF.Exp)
                if not is_sample:
                    act(R("Einv", pp * n, (pp + 1) * n, r=True), bk.v(0, n), AF.Exp, scale=-1.0)
            def s0_decay():
                for pp_ in range(2):
                    for piece in range(4):
                        lo = pp_ * NS * 128 + piece * 512
                        s0 = R("S0", lo, lo + 512)
                        abc = big.t[:, L["E"][0] + pp_ * n + piece * 4: L["E"][0] + pp_ * n + piece * 4 + 4].unsqueeze(2).to_broadcast([128, 4, 128])
                        P.op("dve", (lambda o, i0, i1: lambda e: e.tensor_tensor(out=o.ap, in0=i0.ap, in1=i1, op=ALU.mult))(
                            R("S0", lo, lo + 512, r=True, a=4), R("S0", lo, lo + 512, a=4), abc),
                            reads=[s0, R("E")], writes=[s0])

            def vsel_build(pp_):
                for hh in range(2):
                    hd = 2 * pp_ + hh
                    vs = "vsel%d" % hh
                    P.op("dve", (lambda o, i0, i1: lambda e: e.tensor_tensor(out=o.ap, in0=i0, in1=i1, op=ALU.mult))(
                        R(vs, 0, NS * 128, p0=0, p1=NS, r=True, a=NS),
                        big.t[0:NS, L["vtm"][0] + hd * 128: L["vtm"][0] + (hd + 1) * 128].unsqueeze(1).to_broadcast([NS, NS, 128]),
                        cst.t[0:NS, C_ID:C_ID + NS].unsqueeze(2).to_broadcast([NS, NS, 128])),
                        reads=[R("vtm"), cst.v(C_ID, C_ID + NS)], writes=[R(vs)])

            if is_sample:
                s0_decay()
            slk = wget(wk_d, 0)
            for s in range(nsub):
                tok = lambda k: Ref("xn", k * TT, k * TT + n, xn.t[:, k * TT + s * rows: k * TT + (s + 1) * rows].bitcast(F32R))
                bkk = nb()
                mm(bkk.v(0, 256, 0, rows), [(tok(k), slk.v(k * 256, (k + 1) * 256, r=True)) for k in range(8)])
                if is_sample:
                    act(R("kdec", 0, 256, p0=0, p1=rows, r=True), bkk.v(0, 256, 0, rows), AF.Copy)
                else:
                    bkd = nb()
                    mm(bkd.v(0, 256), [(csr.v(C_LN, C_LN + 128, r=True), R("sp", s * 256, (s + 1) * 256, r=True))])
                    t = ta()
                    act(t.v(0, 256), bkd.v(0, 256), AF.Exp)
                    tt(R("kdec", s * 256, (s + 1) * 256, r=True), bkk.v(0, 256), t.v(0, 256), ALU.mult)
            for h2 in range(2):
                slv = wget(wv_d, h2)
                for s in range(nsub):
                    tok = lambda k: Ref("xn", k * TT, k * TT + n, xn.t[:, k * TT + s * rows: k * TT + (s + 1) * rows].bitcast(F32R))
                    bkv = nb()
                    mm(bkv.v(0, 256, 0, rows), [(tok(k), slv.v(k * 256, (k + 1) * 256, r=True)) for k in range(8)])
                    if h2 == 0:
                        P.op("dve", (lambda o, i: lambda e: e.tensor_copy(out=o.ap, in_=i.ap))(
                            R("vtm", s * 512 + h2 * 256, s * 512 + (h2 + 1) * 256, p0=0, p1=rows, r=True), bkv.v(0, 256, 0, rows)),
                            reads=[bkv.v(0, 256, 0, rows)], writes=[R("vtm", s * 512 + h2 * 256, s * 512 + (h2 + 1) * 256)])
                    else:
                        act(R("vtm", s * 512 + h2 * 256, s * 512 + (h2 + 1) * 256, p0=0, p1=rows, r=True), bkv.v(0, 256, 0, rows), AF.Copy)
            if is_sample:
                vsel_build(0)
            def conv_chunk(c):
                sla = wget(wca_d, c)
                slb = wget(wcb_d, c)
                bh, bc_ = nb(), nb()
                mm(bh.v(0, n), [(sla.v(k * 128, (k + 1) * 128, r=True), xk(k)) for k in range(8)])
                mm(bc_.v(0, n), [(sla.v(1024 + k * 128, 1024 + (k + 1) * 128, r=True), xk(k)) for k in range(8)])
                t = ta()
                act(t.v(0, n), bh.v(0, n), AF.Copy)
                ux = "ucx%d" % (c % 2)
                tt(R(ux, 2, 2 + n, r=True), bc_.v(0, n), t.v(0, n), ALU.mult)
                w0, w1, w2 = (sm.v(1 + c * 3 + j, 2 + c * 3 + j) for j in range(3))
                y1, y2 = tb(), tb()
                if not is_sample:
                    act(R(ux, 0, 2, r=True), carry.v(c * 2, c * 2 + 2), AF.Copy)
                    P.op("act", (lambda o, i, w: lambda e: e.activation(out=o.ap, in_=i.ap, func=AF.Identity, scale=w.ap))(
                        y1.v(0, n), R(ux, 0, n), w0), reads=[R(ux, 0, n), w0], writes=[y1.v(0, n)])
                    stt(y2.v(0, n), R(ux, 1, 1 + n), w1, y1.v(0, n), ALU.mult, ALU.add)
                    stt(y1.v(0, n), R(ux, 2, 2 + n), w2, y2.v(0, n), ALU.mult, ALU.add)
                    act(carry.v(c * 2, c * 2 + 2), R(ux, n, n + 2), AF.Copy)
                else:
                    c0 = R("sc", c * 2 * NS, c * 2 * NS + NS)
                    c1 = R("sc", c * 2 * NS + NS, (c + 1) * 2 * NS)
                    P.op("act", (lambda o, i, w: lambda e: e.activation(out=o.ap, in_=i.ap, func=AF.Identity, scale=w.ap))(
                        y1.v(0, n), c0, w0), reads=[c0, w0], writes=[y1.v(0, n)])
                    stt(y2.v(0, n), c1, w1, y1.v(0, n), ALU.mult, ALU.add)
                    stt(y1.v(0, n), R(ux, 2, 2 + n), w2, y2.v(0, n), ALU.mult, ALU.add)
                    P.op("dve", (lambda o, i: lambda e: e.tensor_copy(out=o.ap, in_=i.ap))(R("cso", c * 2 * NS, c * 2 * NS + NS, r=True), c1),
                         reads=[c1], writes=[R("cso", c * 2 * NS, c * 2 * NS + NS)])
                    P.op("dve", (lambda o, i: lambda e: e.tensor_copy(out=o.ap, in_=i.ap))(R("cso", c * 2 * NS + NS, (c + 1) * 2 * NS, r=True), R(ux, 2, 2 + n)),
                         reads=[R(ux, 2, 2 + n)], writes=[R("cso", c * 2 * NS + NS, (c + 1) * 2 * NS)])
                bb = nb()
                mm(bb.v(0, n), [(slb.v(k * 128, (k + 1) * 128, r=True), xk(k)) for k in range(8)])
                tt(om.v((4 + c) * TT, (4 + c) * TT + n, r=True), bb.v(0, n), y1.v(0, n), ALU.mult)
            slkk = wget(wkq_d, 0)
            for pp in (() if is_sample else range(2)):
                bk = nb()
                mm(bk.v(0, n), [(slkk.v((pp * 8 + k) * 128, (pp * 8 + k + 1) * 128, r=True), xk(k)) for k in range(8)])
                tt(R("kt", pp * n, (pp + 1) * n, r=True), bk.v(0, n), R("Einv", pp * n, (pp + 1) * n), ALU.mult)
            slq = wget(wkq_d, 1)
            if is_sample:
                zero_r(lambda a_, b_: R("qz", a_, b_, r=True), 0, 4 * NS)
            for pp in range(2):
                bk = nb()
                mm(bk.v(0, n), [(slq.v((pp * 8 + k) * 128, (pp * 8 + k + 1) * 128, r=True), xk(k)) for k in range(8)])
                for hh in range(2):
                    hd = 2 * pp + hh
                    p0, p1 = hh * 64, hh * 64 + 64
                    if not is_sample:
                        stt(qtz.v(hd * TT, hd * TT + n, p0, p1, r=True), bk.v(0, n, p0, p1), 0.125, R("E", pp * n, (pp + 1) * n, p0=p0, p1=p1),
                            ALU.mult, ALU.mult)
                    else:
                        P.op("dve", (lambda o, i: lambda e: e.tensor_scalar(out=o.ap, in0=i.ap, scalar1=0.125, scalar2=0.0, op0=ALU.mult, op1=ALU.add))(
                            R("qz", hd * NS, (hd + 1) * NS, p0=p0, p1=p1, r=True), bk.v(0, n, p0, p1)),
                            reads=[bk.v(0, n, p0, p1)], writes=[R("qz", hd * NS, (hd + 1) * NS)])
            for hd in range(4):
                if hd % 2 == 0:
                    slg = wget(wg_d, hd // 2)
                bgk = nb()
                mm(bgk.v(0, n), [(slg.v(((hd % 2) * 8 + k) * 128, ((hd % 2) * 8 + k + 1) * 128, r=True), xk(k)) for k in range(8)])
                act(om.v(hd * TT, hd * TT + n, r=True), bgk.v(0, n), AF.Silu)
                if hd == 3:
                    act(junk.v(0, 1, 0, 1), eps_ap_row, AF.Ln)
            if is_sample:
                for c in range(4):
                    conv_chunk(c)
            if not is_sample:
                for s in range(nsub):
                    c0_, c1_ = s * 128, (s + 1) * 128
                    ba = nb()
                    groups = []
                    for hd in range(4):
                        pp = hd // 2
                        groups.append((ba.v(hd * 128, (hd + 1) * 128),
                                       [(R("kt", pp * n + c0_, pp * n + c1_, r=True), qtz.v(hd * TT + c0_, hd * TT + c1_, r=True))]))
                    mm_multi(groups, ba.v(0, 512))
                    am = "attm%d" % (s % 2)
                    P.op("dve", (lambda o, i, m_: lambda e: e.tensor_tensor(out=o.ap, in0=i.ap, in1=m_, op=ALU.mult))(
                        R(am, 0, 512, r=True, a=4), ba.v(0, 512, a=4), cst.v(C_MK, C_MK + 128).ap.unsqueeze(1).to_broadcast([128, 4, 128])),
                        reads=[ba.v(0, 512), cst.v(C_MK, C_MK + 128)], writes=[R(am, 0, 512)])
                    conv_chunk(s)
                    bo = nb()
                    groups = []
                    for hd in range(4):
                        pp = hd // 2
                        groups.append((bo.v(hd * 128, (hd + 1) * 128),
                                       [(Sr[s % 2].v(pp * 128, (pp + 1) * 128, r=True), qtz.v(hd * TT + c0_, hd * TT + c1_, r=True)),
                                        (R("vtm", s * 512 + hd * 128, s * 512 + (hd + 1) * 128, r=True), R(am, hd * 128, (hd + 1) * 128, r=True))]))
                    mm_multi(groups, bo.v(0, 512))
                    P.op("act", (lambda o, i: lambda e: e.activation(out=o.ap, in_=i.ap, func=AF.Copy))(
                        R("ofm", 0, 4 * n, r=True, a=4, sl=(c0_, c1_)), bo.v(0, 512, a=4)), reads=[bo.v(0, 512)], writes=[R("ofm", 0, 4 * n)])
                    oq = "osq%d" % (s % 2)
                    act(R(oq, 0, 512, r=True), bo.v(0, 512), AF.Square)
                    bu = nb()
                    groups = []
                    for pp in range(2):
                        groups.append((bu.v(pp * 256, (pp + 1) * 256),
                                       [(R("kdec", s * 256 + pp * 128, s * 256 + (pp + 1) * 128, r=True),
                                         R("vtm", s * 512 + pp * 256, s * 512 + (pp + 1) * 256, r=True))]))
                    mm_multi(groups, bu.v(0, 512))
                    for pp in range(2):
                        for hh in range(2):
                            p0, p1 = hh * 64, hh * 64 + 64
                            stt(Sst.v(pp * 128, (pp + 1) * 128, p0, p1), Sst.v(pp * 128, (pp + 1) * 128, p0, p1),
                                R("E", pp * n + c1_ - 1, pp * n + c1_, p0=p0, p1=p1),
                                bu.v(pp * 256 + hh * 128, pp * 256 + (hh + 1) * 128, p0, p1), ALU.mult, ALU.add)
                    act(Sr[(s + 1) % 2].v(0, 256, r=True), Sst.v(0, 256), AF.Copy)
                    bq = nb()
                    mm(bq.v(0, 512), [(csr.v(C_OV, C_OV + 128, r=True), R(oq, 0, 512, r=True))])
                    tl = ta()
                    act(tl.v(0, 512), bq.v(0, 512), AF.Ln, extra=[eps_ap], bias=eps_ap.ap)
                    rsh = tb()
                    act(rsh.v(0, 512), tl.v(0, 512), AF.Exp, scale=-0.5)
                    t2 = ta()
                    P.op("dve", (lambda o, i0, w, i1: lambda e: e.scalar_tensor_tensor(out=o.ap, in0=i0.ap, scalar=w.ap, in1=i1.ap, op0=ALU.mult, op1=ALU.mult))(
                        t2.v(0, 512, a=4), R("ofm", 0, 4 * n, a=4, sl=(c0_, c1_)), sm.v(0, 1), rsh.v(0, 512, a=4)),
                        reads=[R("ofm", 0, 4 * n), sm.v(0, 1), rsh.v(0, 512)], writes=[t2.v(0, 512)])
                    P.op("dve", (lambda o, i0, i1: lambda e: e.tensor_tensor(out=o.ap, in0=i0.ap, in1=i1.ap, op=ALU.mult))(
                        om.v(0, 4 * TT, r=True, a=4, sl=(c0_, c1_)), om.v(0, 4 * TT, a=4, sl=(c0_, c1_)), t2.v(0, 512, a=4)),
                        reads=[om.v(0, 4 * TT), t2.v(0, 512)], writes=[om.v(0, 4 * TT)])
                if last_prompt:
                    dma_out("o_glap", glap_d.rearrange("a p v -> p a v"), Sst.v(0, 256, a=2))
                    bk = nb()
                    items = []
                    for c in range(4):
                        items.append((bk.v(c * 128, (c + 1) * 128, 0, 2), carry.v(c * 2, c * 2 + 2), ident()))
                    transposes(items, bk.v(0, 512), [carry.v(0, 8), ident()])
                    t = tb()
                    act(t.v(0, 512, 0, 2), bk.v(0, 512, 0, 2), AF.Copy)
                    dma_out("o_convp", convp_d[:, :], t.v(0, 512, 0, 2))
            else:
                for pp in range(2):
                    if pp == 1:
                        vsel_build(1)
                    for piece in range(4):
                        bA, bB = nb(), nb()
                        lhsT = R("kdec", pp * 128, (pp + 1) * 128, p0=0, p1=NS, r=True)
                        mm(bA.v(0, 512), [(lhsT, R("vsel0", piece * 512, (piece + 1) * 512, p0=0, p1=NS, r=True))])
                        mm(bB.v(0, 512), [(lhsT, R("vsel1", piece * 512, (piece + 1) * 512, p0=0, p1=NS, r=True))])
                        lo = pp * NS * 128 + piece * 512
                        tt(R("S0", lo, lo + 512, p0=0, p1=64, r=True), R("S0", lo, lo + 512, p0=0, p1=64), bA.v(0, 512, 0, 64), ALU.add)
                        tt(R("S0", lo, lo + 512, p0=64, p1=128, r=True), R("S0", lo, lo + 512, p0=64, p1=128), bB.v(0, 512, 64, 128), ALU.add)
                bo = nb()
                groups = []
                for hd in range(4):
                    pp = hd // 2
                    for b in range(NS):
                        j = hd * NS + b
                        groups.append((bo.v(2 * j, 2 * j + 2),
                                       [(R("S0", pp * NS * 128 + b * 128, pp * NS * 128 + (b + 1) * 128, r=True),
                                         R("qz", j, j + 2, r=True))]))
                mm_multi(groups, bo.v(0, 8 * NS))
                bo2 = Ref(bo.name, 0, 8 * NS, bo.t[:, 0:8 * NS].rearrange("p (j t) -> p j t", t=2)[:, :, 0])
                bo3 = Ref(bo.name, 0, 8 * NS, bo.t[:, 0:8 * NS].rearrange("p (a b t) -> p a b t", a=4, t=2)[:, :, :, 0])
                act(R("ofm", 0, 4 * n, r=True), bo2, AF.Copy)
                act(R("osqs", r=True), bo2, AF.Square)
                dma_out("o_glas", glas_d.rearrange("a p f -> p a f"), R("S0", 0, 2 * NS * 128, a=2))
                for j in range(2):
                    bk = nb()
                    items = []
                    for c in range(4):
                        items.append((bk.v(c * 128, (c + 1) * 128, 0, NS), R("cso", c * 2 * NS + j * NS, c * 2 * NS + (j + 1) * NS), ident()))
                    transposes(items, bk.v(0, 512), [R("cso"), ident()])
                    act(R("cstg", j * 512, (j + 1) * 512, p0=0, p1=NS, r=True), bk.v(0, 512, 0, NS), AF.Copy)
                dma_out("o_convs", convs_d[:, :], R("cstg", 0, 1024, p0=0, p1=NS))
            if is_sample:
                bk = nb()
                mm(bk.v(0, 4 * n), [(csr.v(C_OV, C_OV + 128, r=True), R("osqs", r=True))])
                t = ta()
                act(t.v(0, 4 * n), bk.v(0, 4 * n), AF.Ln, extra=[eps_ap], bias=eps_ap.ap)
                rsh = tb()
                act(rsh.v(0, 4 * n), t.v(0, 4 * n), AF.Exp, scale=-0.5)
                t2 = ta()
                stt(t2.v(0, 4 * n), R("ofm", 0, 4 * n), sm.v(0, 1), rsh.v(0, 4 * n), ALU.mult, ALU.mult)
                P.op("dve", (lambda o, i0, i1: lambda e: e.tensor_tensor(out=o.ap, in0=i0.ap, in1=i1.ap, op=ALU.mult))(
                    om.v(0, 4 * TT, r=True, a=4, sl=(0, n)), om.v(0, 4 * TT, a=4, sl=(0, n)), t2.v(0, 4 * n, a=4)),
                    reads=[om.v(0, 4 * TT), t2.v(0, 4 * n)], writes=[om.v(0, 4 * TT)])
            for m in range(8):
                if m % 2 == 0:
                    slo = wget(wo_d, m // 2)
                bk = nb()
                mm(bk.v(0, n), [(slo.v(((m % 2) * 8 + k) * 128, ((m % 2) * 8 + k + 1) * 128, r=True), om.v(k * TT, k * TT + n, r=True)) for k in range(8)])
                fo_evac(3, xn, m, bk, n, 1.0)
            post_norm(xn, n)

        def ple_p(n, t0):
            for k2 in range(2):
                dst = hgb[k2].v(0, n, r=True)
                src = p_d[k2 * 128:(k2 + 1) * 128, t0:t0 + n]
                P.op("pool", (lambda d_, s_: lambda e: e.dma_start(out=d_.ap, in_=s_))(dst, src), writes=[dst], dma="pin%d" % k2)

        def ple_stage(n, t0, next_t0=None):
            pre_norm_deferred(6, n, stats=False)
            slg0 = wget(wpg_d, 0)
            slp0 = wget(wpp_d, 0)
            wbk = [nb(), nb()]
            for k in range(8):
                def fnk(e, k=k):
                    for m_ in range(2):
                        ins = e.matmul(wbk[m_].v(0, n).ap, slg0.v((m_ * 8 + k) * 128, (m_ * 8 + k + 1) * 128, r=True).ap,
                                       xn.v(k * TT, k * TT + n, r=True).ap, start=(k == 0), stop=(k == 7))
                    return ins
                P.op("pe", fnk, reads=[xn.v(k * TT, k * TT + n), slg0.v(0, 2048)], writes=[b_.v(0, n) for b_ in wbk])
            norm_stats(n, 8, C_OD)
            for m in range(8):
                if m < 2:
                    slg, slp = slg0, slp0
                    bkg = wbk[m]
                else:
                    if m % 2 == 0:
                        slg = wget(wpg_d, m // 2)
                        slp = wget(wpp_d, m // 2)
                    bkg = nb()
                    mm(bkg.v(0, n), [(slg.v(((m % 2) * 8 + k) * 128, ((m % 2) * 8 + k + 1) * 128, r=True), xn.v(k * TT, k * TT + n, r=True)) for k in range(8)])
                bkp = nb()
                mm(bkp.v(0, n), [(slp.v(((m % 2) * 2 + k2) * 128, ((m % 2) * 2 + k2 + 1) * 128, r=True), hgb[k2].v(0, n, r=True)) for k2 in range(2)])
                t = ta()
                tt(t.v(0, n), bkg.v(0, n), rs.v(0, n), ALU.mult)
                act(t.v(0, n), t.v(0, n), AF.Sigmoid)
                if m == 7:
                    act(junk.v(0, 1, 0, 1), eps_ap_row, AF.Ln)
                tt(om.v(m * TT, m * TT + n, r=True), bkp.v(0, n), t.v(0, n), ALU.mult)
                act(sq.v(m * TT, m * TT + n, r=True), om.v(m * TT, m * TT + n), AF.Square)
            norm_stats(n, 8, C_OD)
            def m1(c):
                stt(om.v(c * TT, c * TT + n, r=True), om.v(c * TT, c * TT + n), ngl.v(7 * 8 + c, 7 * 8 + c + 1), rs.v(0, n), ALU.mult, ALU.mult)
            def ad(c):
                tt(hc(c, n), hc(c, n), om.v(c * TT, c * TT + n), ALU.add)
                dma_out("oy%d_%s" % (c, CUR["name"]), y_d[c * 128:(c + 1) * 128, t0:t0 + n], hc(c, n))
            seq = [("m", 0), ("m", 1), ("a", 0), ("m", 2), ("a", 1), ("m", 3), ("a", 2), ("m", 4), ("a", 3), ("m", 5), ("a", 4),
                   ("m", 6), ("a", 5), ("m", 7), ("a", 6), ("a", 7)]
            for k_, c in seq:
                (m1 if k_ == "m" else ad)(c)

        ntile = SEQ // TT
        SETS = [{"hcf": _hcA, "name": "A"}, {"hcf": _hcB, "name": "B"}]
        SAMPLE = {"hcf": _hcS, "name": "S"}
        CUR.update(SETS[0])
        stage_in(TT, 0)
        CUR.update(SAMPLE)
        stage_in(NS, SEQ)
        CUR.update(SETS[0])
        for ti in range(ntile):
            t0 = ti * TT
            last = ti == ntile - 1
            if not last:
                CUR.update(SETS[(ti + 1) % 2])
                stage_in(TT, t0 + TT)
            PROMPT = SETS[ti % 2]
            CUR.update(PROMPT)
            ffn(0, 0, 1, TT, samp=SampFFN(0, 0, 1) if last else None, dve_scale=(ti == 0))
            mixer(TT, False, last)
            if last:
                CUR.update(SAMPLE)
                mixer(NS, True, False)
                CUR.update(PROMPT)
            ple_p(TT, t0)
            ffn(1, 4, 5, TT, samp=SampFFN(1, 4, 5) if last else None, dve_scale=last)
            ple_stage(TT, t0)
            if last:
                CUR.update(SAMPLE)
                ple_p(NS, SEQ)
                ple_stage(NS, SEQ)
                CUR.update(PROMPT)
        assert wq["used"] == len(wq["order"])
        P.finish_waits("sp")
        P.emit(block)
    return nc


def _chunks_km(w, cols):
    K = w.shape[0] // 128
    return np.ascontiguousarray(w[:, cols].reshape(K, 128, len(cols)).transpose(1, 0, 2))


def _prep_shared(inp):
    f32 = np.float32
    sh = {}
    cst = np.zeros((128, C_W), f32)
    csr = np.zeros((128, CR_W), f32)
    cst[:, C_ID:C_ID + 128] = np.eye(128)
    csr[:, C_OD:C_OD + 128] = 1.0 / D
    csr[:, C_OV:C_OV + 128] = 1.0 / 128
    tri = np.triu(np.ones((128, 128)))
    csr[:, C_UN:C_UN + 128] = tri * (-1.0 / 16)
    csr[:, C_LN:C_LN + 128] = (1.0 - tri) * (-1.0 / 16)
    cst[:, C_MK:C_MK + 128] = tri
    csr[:, C_NI:C_NI + 128] = np.eye(128) * (-1.0 / 16)
    cst[:, C_ONE:C_ONE + 128] = 1.0
    cst[:, C_EPS] = EPS
    sh["cst"] = cst
    sh["csr"] = csr
    ng = inp["norm_g"][0]
    sh["ngl"] = np.ascontiguousarray(ng.reshape(8, 8, 128).transpose(2, 0, 1).reshape(128, 64))
    sm = np.zeros((128, 16), f32)
    sm[:, 0] = inp["gla_norm_g"][0]
    cw = inp["conv_w"][0]
    sm[:, 1:13] = cw.reshape(3, 4, 128).transpose(2, 1, 0).reshape(128, 12)
    sh["sm"] = sm
    sh["wgu"] = np.ascontiguousarray(inp["w_gate_up"][0])
    sh["bg"] = np.ascontiguousarray(inp["b_gate"][0].reshape(1, 256))
    w_in = inp["w_in"][0]
    ar = np.arange
    sh["wr"] = _chunks_km(w_in, ar(1536, 1552)).reshape(128, 128)
    for f, nm in ((1, "ffn1"), (2, "ffn2")):
        g, u, d = inp[nm + "_gate"][0], inp[nm + "_up"][0], inp[nm + "_down"][0]
        a = np.empty((NJ, 128, 2, 8, 128), f32)
        a[:, :, 0] = g.reshape(8, 128, NJ, 128).transpose(2, 1, 0, 3)
        a[:, :, 1] = u.reshape(8, 128, NJ, 128).transpose(2, 1, 0, 3)
        sh[f"f{f}a"] = a.reshape(NJ * 128, 2048)
        sh[f"f{f}b"] = np.ascontiguousarray(d.reshape(2, 11, 128, 8, 128).transpose(3, 0, 2, 1, 4)).reshape(16 * 128, 1408)
    sh["wk"] = _chunks_km(w_in, ar(256, 512)).reshape(128, 2048)
    sh["wv"] = np.concatenate([_chunks_km(w_in, ar(512 + h2 * 256, 512 + (h2 + 1) * 256)).reshape(128, 2048) for h2 in range(2)], 0)

    def pair(w, c0a, c0b):
        return np.concatenate([_chunks_km(w, ar(c0a, c0a + 128)), _chunks_km(w, ar(c0b, c0b + 128))], axis=1).reshape(128, 2048)
    CB, CC, CH = 1552, 2064, 2576
    sh["wca"] = np.concatenate([pair(w_in, CH + c * 128, CC + c * 128) for c in range(4)], 0)
    sh["wcb"] = np.concatenate([_chunks_km(w_in, ar(CB + c * 128, CB + (c + 1) * 128)).reshape(128, 1024) for c in range(4)], 0)
    sh["wkq"] = np.concatenate([pair(w_in, 256, 384), pair(w_in, 0, 128)], 0)
    sh["wg"] = np.concatenate([pair(w_in, 1024 + i * 256, 1024 + i * 256 + 128) for i in range(2)], 0)
    wo = inp["w_out"][0]
    sh["wo"] = np.concatenate([pair(wo, m2 * 256, m2 * 256 + 128) for m2 in range(4)], 0)
    wpg = inp["w_ple_gate"][0]
    sh["wpg"] = np.concatenate([pair(wpg, m2 * 256, m2 * 256 + 128) for m2 in range(4)], 0)
    wpp = inp["w_ple_proj"][0]
    sh["wpp"] = np.ascontiguousarray(wpp.reshape(2, 128, 4, 2, 128).transpose(2, 1, 3, 0, 4)).reshape(4 * 128, 512)
    return {k: np.ascontiguousarray(v, dtype=f32) for k, v in sh.items()}


_NC_CACHE = {}


def kernel(**inputs):
    inp = {k: np.asarray(v) for k, v in inputs.items()}
    shared = _prep_shared(inp)
    in_maps = []
    for c in range(NCORES):
        m = dict(shared)
        m["x"] = np.ascontiguousarray(np.concatenate([inp["x_prompt"][c], inp["x_sample"][c * NS:(c + 1) * NS, 0]], 0).T, dtype=np.float32)
        m["p"] = np.ascontiguousarray(np.concatenate([inp["p_prompt"][0, c], inp["p_sample"][0, c * NS:(c + 1) * NS, 0]], 0).T, dtype=np.float32)
        sg = inp["state_gla"][0, c * NS:(c + 1) * NS]
        m["sg"] = np.ascontiguousarray(sg.reshape(NS, 2, 128, 128).transpose(2, 1, 0, 3).reshape(128, 2 * NS * 128), dtype=np.float32)
        sc = inp["state_conv"][0, c * NS:(c + 1) * NS]
        m["sc"] = np.ascontiguousarray(sc.reshape(NS, 2, 4, 128).transpose(3, 2, 1, 0).reshape(128, 4 * 2 * NS), dtype=np.float32)
        in_maps.append(m)
    if "nc" not in _NC_CACHE:
        _NC_CACHE["nc"] = build_program()
    res = run_bass_kernel_spmd(_NC_CACHE["nc"], in_maps, core_ids=list(range(NCORES)))
    R = res.results
    y_p = np.stack([np.ascontiguousarray(R[c]["y"][:, :SEQ].T) for c in range(NCORES)], 0).astype(np.float32)
    y_s = np.concatenate([R[c]["y"][:, SEQ:].T for c in range(NCORES)], 0).reshape(NCORES * NS, 1, D).astype(np.float32)
    gla_p = np.stack([R[c]["gla_p"].reshape(4, 64, 128) for c in range(NCORES)], 0)[None].astype(np.float32)
    conv_p = np.stack([R[c]["conv_p"] for c in range(NCORES)], 0)[None].astype(np.float32)
    gla_s = np.concatenate([R[c]["gla_s"].reshape(2, 128, NS, 128).transpose(2, 0, 1, 3).reshape(NS, 4, 64, 128)
                            for c in range(NCORES)], 0)[None].astype(np.float32)
    conv_s = np.concatenate([R[c]["conv_s"].reshape(NS, 2, 512) for c in range(NCORES)], 0)[None].astype(np.float32)
    return (y_p, y_s, gla_p, conv_p, gla_s, conv_s)
```
